# Optimizing a Trainium2 kernel written in Bass

```python
import jax, jax.numpy as jnp
from jax import lax
import numpy as np

D_MODEL = 1024
BATCH = 32
SEQ = 2048
DEPTH = 1
DEC_BATCH = 16
DEC_SEQ = 16
PAST_LEN = 2048

CHUNK = 64
HEAD_DIM = 64
ATTN_WIDTH = D_MODEL // 2
N_Q_HEADS = ATTN_WIDTH // HEAD_DIM
N_KV_HEADS = 2
GROUP = N_Q_HEADS // N_KV_HEADS
KV_WIDTH = N_KV_HEADS * HEAD_DIM
WINDOW = 128
WINDOW_CHUNKS = WINDOW // CHUNK
CONV_CH = D_MODEL - ATTN_WIDTH
CONV_KERNEL = 31
D_FF = 4 * D_MODEL
PLE_DIM = 256
EPS = 1e-6
SPLITS = [ATTN_WIDTH, ATTN_WIDTH + KV_WIDTH, ATTN_WIDTH + 2 * KV_WIDTH, ATTN_WIDTH + 2 * KV_WIDTH + CONV_CH]
IN_WIDTH = ATTN_WIDTH + 2 * KV_WIDTH + 2 * CONV_CH

kernel_name = "hymba_swa_sink_conformer_stream_step"


def rmsnorm(x, g):
    xf = x.astype(jnp.float32)
    y = xf * lax.rsqrt(jnp.mean(xf * xf, axis=-1, keepdims=True) + EPS)
    return (y * g.astype(jnp.float32)).astype(x.dtype)


def layernorm(x, g, b):
    xf = x.astype(jnp.float32)
    mu = jnp.mean(xf, axis=-1, keepdims=True)
    xc = xf - mu
    y = xc * lax.rsqrt(jnp.mean(xc * xc, axis=-1, keepdims=True) + EPS)
    return (y * g.astype(jnp.float32) + b.astype(jnp.float32)).astype(x.dtype)


def sink_attend(q, k, v, sinks, valid):
    s = jnp.einsum('...qhgd,...khd->...hgqk', q, k, preferred_element_type=jnp.float32) * (HEAD_DIM ** -0.5)
    s = jnp.where(valid, s, -jnp.inf)
    sink = sinks.astype(jnp.float32).reshape(N_KV_HEADS, GROUP, 1, 1)
    m = jnp.maximum(jnp.max(s, axis=-1, keepdims=True), sink)
    e = jnp.exp(s - m)
    p = e / (jnp.sum(e, axis=-1, keepdims=True) + jnp.exp(sink - m))
    return jnp.einsum('...hgqk,...khd->...qhgd', p.astype(v.dtype), v)


def swa_prompt(q, k, v, sinks):
    B, S = q.shape[0], q.shape[1]
    nc = S // CHUNK
    qb = q.reshape(B, nc, CHUNK, N_KV_HEADS, GROUP, HEAD_DIM)
    pad = ((0, 0), (WINDOW, 0), (0, 0), (0, 0))
    kp = jnp.pad(k, pad).reshape(B, nc + WINDOW_CHUNKS, CHUNK, N_KV_HEADS, HEAD_DIM)
    vp = jnp.pad(v, pad).reshape(B, nc + WINDOW_CHUNKS, CHUNK, N_KV_HEADS, HEAD_DIM)
    kb = jnp.concatenate([kp[:, j:j + nc] for j in range(WINDOW_CHUNKS + 1)], axis=2)
    vb = jnp.concatenate([vp[:, j:j + nc] for j in range(WINDOW_CHUNKS + 1)], axis=2)
    key_pos = jnp.arange(nc)[:, None] * CHUNK - WINDOW + jnp.arange(WINDOW + CHUNK)[None, :]
    valid = (key_pos >= 0)[:, None, None, None, :]
    o = sink_attend(qb, kb, vb, sinks, valid)
    return o.reshape(B, S, ATTN_WIDTH)


def swa_sample(q, k, v, k_cache, v_cache, sinks):
    B, T = q.shape[0], q.shape[1]
    kf = jnp.concatenate([k_cache.astype(k.dtype), k], axis=1)
    vf = jnp.concatenate([v_cache.astype(v.dtype), v], axis=1)
    o = sink_attend(q, kf, vf, sinks, True)
    return o.reshape(B, T, ATTN_WIDTH)


def depthwise_causal(u, w, b, left):
    if left is None:
        u_pad = jnp.pad(u, ((0, 0), (CONV_KERNEL - 1, 0), (0, 0)))
    else:
        u_pad = jnp.concatenate([left.astype(u.dtype), u], axis=1)
    y = lax.conv_general_dilated(u_pad, w[:, None, :].astype(u.dtype), window_strides=(1,), padding='VALID',
                                 dimension_numbers=('NWC', 'WIO', 'NWC'), feature_group_count=CONV_CH)
    return y + b, u_pad[:, -(CONV_KERNEL - 1):]


def layer(x, p_i, k_cache, v_cache, conv_cache, prm, i):
    B, T = x.shape[0], x.shape[1]
    hn = rmsnorm(x, prm['norm_mix'][i])
    proj = hn @ prm['w_in'][i]
    q, k, v, a, g = jnp.split(proj, SPLITS, axis=-1)
    q = q.reshape(B, T, N_KV_HEADS, GROUP, HEAD_DIM)
    k = k.reshape(B, T, N_KV_HEADS, HEAD_DIM)
    v = v.reshape(B, T, N_KV_HEADS, HEAD_DIM)
    sinks = prm['sinks'][i]
    if k_cache is None:
        attn = swa_prompt(q, k, v, sinks)
        new_k, new_v = k[:, -WINDOW:], v[:, -WINDOW:]
    else:
        attn = swa_sample(q, k, v, k_cache, v_cache, sinks)
        new_k, new_v = k, v
    u = a * jax.nn.sigmoid(g)
    c, new_conv = depthwise_causal(u, prm['conv_w'][i], prm['conv_b'][i], conv_cache)
    c = jax.nn.silu(layernorm(c, prm['ln_g'][i], prm['ln_b'][i]))
    merged = jnp.concatenate([rmsnorm(attn, prm['attn_out_g'][i]), rmsnorm(c, prm['conv_out_g'][i])], axis=-1)
    x = x + merged @ prm['w_out'][i]
    hf = rmsnorm(x, prm['norm_ffn'][i])
    x = x + jnp.square(jax.nn.relu(hf @ prm['w_up'][i])) @ prm['w_down'][i]
    gate = jax.nn.sigmoid(rmsnorm(x, prm['norm_ple'][i]) @ prm['w_ple_gate'][i])
    x = x + (p_i @ prm['w_ple'][i]) * gate
    return x, new_k, new_v, new_conv


def setup_inputs(seed: int = 0) -> dict:
    key = jax.random.key(seed)
    ks = jax.random.split(key, 32)
    f32 = jnp.float32

    def nrm(k, shape, scale):
        return jax.random.normal(k, shape, f32) * scale

    def gain(k, shape):
        return 1.0 + 0.05 * jax.random.normal(k, shape, f32)

    cache_rows = min(WINDOW, PAST_LEN)
    return {
        'x_prompt': nrm(ks[0], (BATCH, SEQ, D_MODEL), 1.0),
        'x_sample': nrm(ks[1], (DEC_BATCH, DEC_SEQ, D_MODEL), 1.0),
        'p_prompt': nrm(ks[2], (DEPTH, BATCH, SEQ, PLE_DIM), 1.0),
        'p_sample': nrm(ks[3], (DEPTH, DEC_BATCH, DEC_SEQ, PLE_DIM), 1.0),
        'cache_k': nrm(ks[4], (DEPTH, DEC_BATCH, cache_rows, N_KV_HEADS, HEAD_DIM), 1.0),
        'cache_v': nrm(ks[5], (DEPTH, DEC_BATCH, cache_rows, N_KV_HEADS, HEAD_DIM), 1.0),
        'state_conv': nrm(ks[6], (DEPTH, DEC_BATCH, CONV_KERNEL - 1, CONV_CH), 0.5),
        'norm_mix': gain(ks[7], (DEPTH, D_MODEL)),
        'w_in': nrm(ks[8], (DEPTH, D_MODEL, IN_WIDTH), D_MODEL ** -0.5),
        'sinks': nrm(ks[9], (DEPTH, N_Q_HEADS), 0.5),
        'conv_w': nrm(ks[10], (DEPTH, CONV_KERNEL, CONV_CH), CONV_KERNEL ** -0.5),
        'conv_b': nrm(ks[11], (DEPTH, CONV_CH), 0.02),
        'ln_g': gain(ks[12], (DEPTH, CONV_CH)),
        'ln_b': nrm(ks[13], (DEPTH, CONV_CH), 0.02),
        'attn_out_g': gain(ks[14], (DEPTH, ATTN_WIDTH)),
        'conv_out_g': gain(ks[15], (DEPTH, CONV_CH)),
        'w_out': nrm(ks[16], (DEPTH, D_MODEL, D_MODEL), D_MODEL ** -0.5),
        'norm_ffn': gain(ks[17], (DEPTH, D_MODEL)),
        'w_up': nrm(ks[18], (DEPTH, D_MODEL, D_FF), D_MODEL ** -0.5),
        'w_down': nrm(ks[19], (DEPTH, D_FF, D_MODEL), D_FF ** -0.5),
        'norm_ple': gain(ks[20], (DEPTH, D_MODEL)),
        'w_ple_gate': nrm(ks[21], (DEPTH, D_MODEL, D_MODEL), D_MODEL ** -0.5),
        'w_ple': nrm(ks[22], (DEPTH, PLE_DIM, D_MODEL), PLE_DIM ** -0.5),
        'final_norm': gain(ks[23], (D_MODEL,)),
    }


def reference(x_prompt, x_sample, p_prompt, p_sample, cache_k, cache_v, state_conv,
              norm_mix, w_in, sinks, conv_w, conv_b, ln_g, ln_b, attn_out_g, conv_out_g, w_out,
              norm_ffn, w_up, w_down, norm_ple, w_ple_gate, w_ple, final_norm):
    prm = dict(norm_mix=norm_mix, w_in=w_in, sinks=sinks, conv_w=conv_w, conv_b=conv_b, ln_g=ln_g, ln_b=ln_b,
               attn_out_g=attn_out_g, conv_out_g=conv_out_g, w_out=w_out, norm_ffn=norm_ffn, w_up=w_up,
               w_down=w_down, norm_ple=norm_ple, w_ple_gate=w_ple_gate, w_ple=w_ple)
    hp, hs = x_prompt, x_sample
    kp_l, vp_l, cp_l, ks_l, vs_l, cs_l = [], [], [], [], [], []
    for i in range(DEPTH):
        hp, kp_i, vp_i, cp_i = layer(hp, p_prompt[i], None, None, None, prm, i)
        hs, ks_i, vs_i, cs_i = layer(hs, p_sample[i], cache_k[i], cache_v[i], state_conv[i], prm, i)
        kp_l.append(kp_i); vp_l.append(vp_i); cp_l.append(cp_i)
        ks_l.append(ks_i); vs_l.append(vs_i); cs_l.append(cs_i)
    y_prompt = rmsnorm(hp, final_norm)
    y_sample = rmsnorm(hs, final_norm)
    new_k_prompt = jnp.stack(kp_l)
    new_v_prompt = jnp.stack(vp_l)
    new_conv_prompt = jnp.stack(cp_l)
    new_k_sample = jnp.stack(ks_l)
    new_v_sample = jnp.stack(vs_l)
    new_conv_sample = jnp.stack(cs_l)
    return (y_prompt, y_sample, new_k_prompt, new_v_prompt, new_conv_prompt, new_k_sample, new_v_sample, new_conv_sample)
```

```python
import os
import numpy as np
from contextlib import ExitStack
import concourse.bass as bass
import concourse.mybir as mybir
from concourse.bass_utils import run_bass_kernel_spmd

F32 = mybir.dt.float32
BF16 = mybir.dt.bfloat16
AF = mybir.ActivationFunctionType
ALU = mybir.AluOpType

ENGS = ("pe", "act", "dve", "pool", "sp")
NCORES = 8
D = 1024
T = 512
SEQ = 2048
NSEQ = 4
NSS = 2
SL = 16
EPS = 1e-6
NSLOT = 5
NPRE = 4
POOL_CONVERT = False
WAIT_ATTACH = 1
B2_END = 0.25
FILLER = True
FILL_MARGIN = 100.0
BLOCK_NS = 3000.0
SYNC_SAME_ENGINE_WAR = True
PACE = 0.8
SYNC_LAT = 250.0
PE_GHZ = 1.95
NDVE_TAPS = 10


class Op:
    __slots__ = ("eng", "emit", "deps", "needs_sig", "sig", "dma", "waits", "eidx", "tag", "wm", "fseq", "cost", "t_end")

    def __init__(self, eng, emit, dma):
        self.eng = eng
        self.emit = emit
        self.dma = dma
        self.deps = {}
        self.needs_sig = False
        self.sig = None
        self.waits = []
        self.eidx = -1
        self.wm = 0
        self.fseq = 0
        self.cost = 0
        self.t_end = 0.0


class Prog:
    def __init__(self, nc, dry=False):
        self.nc = nc
        self.dry = dry
        self.eng_ops = {e: [] for e in ENGS}
        self.bufs = {}
        self.dma_cnt = {}
        self.final_keys = set()
        self.cur_tag = ""
        self.tagmap = None
        self.filler_mode = False
        self.fifo = []
        self.fseq = 0
        self.credit = 0.0
        self.ratio = 1.0
        self.cost_acc = {}
        self.cost_key = None
        self.eng_free = {e: 0.0 for e in ENGS}

    def ready_time(self, reads, writes, eng=None):
        t = 0.0
        bufs = self.bufs
        for k in reads:
            st = bufs.get(k)
            if st is not None and st[0] is not None:
                o = st[0]
                if o.t_end > t and (o.eng != eng or o.dma is not None):
                    t = o.t_end
        for k in writes:
            st = bufs.get(k)
            if st is not None:
                o = st[0]
                if o is not None and o.t_end > t and (o.eng != eng or o.dma is not None):
                    t = o.t_end
                for r in st[1]:
                    if r.t_end > t and (r.eng != eng or r.dma is not None):
                        t = r.t_end
        return t

    def _append(self, op):
        op.eidx = len(self.eng_ops[op.eng])
        self.eng_ops[op.eng].append(op)

    def flush_fifo(self, upto=None):
        fifo = self.fifo
        n = 0
        while n < len(fifo) and (upto is None or fifo[n].fseq <= upto):
            self._append(fifo[n])
            n += 1
        if n:
            del fifo[:n]

    def add(self, eng, emit, reads=(), writes=(), dma=None, final=False, cost=0, dur=None):
        if self.dry:
            return None
        op = Op(eng, emit, dma)
        op.cost = max(cost, 64)
        op.tag = self.cur_tag
        if dur is not None:
            t0 = self.ready_time(reads, writes, eng) + SYNC_LAT
            ef = self.eng_free[eng]
            if ef > t0:
                t0 = ef
            if dma is not None:
                self.eng_free[eng] = t0 + 60.0
            else:
                self.eng_free[eng] = t0 + dur
            op.t_end = t0 + dur
        bufs = self.bufs
        deps = op.deps
        for k in reads:
            st = bufs.get(k)
            if st is not None and st[0] is not None:
                deps[st[0]] = True
        for k in writes:
            st = bufs.get(k)
            if st is not None:
                if st[0] is not None and st[0] not in deps:
                    deps[st[0]] = False
                for r in st[1]:
                    if r not in deps:
                        deps[r] = False
        for k in reads:
            st = bufs.get(k)
            if st is None:
                bufs[k] = [None, [op]]
            else:
                st[1].append(op)
        for k in writes:
            bufs[k] = [op, []]
        deps.pop(op, None)
        self._append(op)
        if dma is not None:
            self.dma_cnt[dma] = self.dma_cnt.get(dma, 0) + 16
            op.sig = self.dma_cnt[dma]
            if final:
                self.final_keys.add(dma)
        return op

    def finalize_and_emit(self):
        nc = self.nc
        for e in ENGS:
            for op in self.eng_ops[e]:
                best = {}
                for d, raw in op.deps.items():
                    if d.dma is not None:
                        key = ("dma", d.dma)
                        cur = best.get(key)
                        if cur is None or d.sig > cur.sig:
                            best[key] = d
                    else:
                        if d.eng == op.eng and op.dma is None:
                            if op.eng == "pe" or (not raw and not SYNC_SAME_ENGINE_WAR):
                                continue
                        key = ("eng", d.eng)
                        cur = best.get(key)
                        if cur is None or d.eidx > cur.eidx:
                            best[key] = d
                op.deps = list(best.values())
                for d in op.deps:
                    d.needs_sig = True
        self.check_no_deadlock()
        for e in ENGS:
            c = 0
            for op in self.eng_ops[e]:
                if op.dma is None and op.needs_sig:
                    c += 1
                    op.sig = c
        with ExitStack() as es:
            esem = {e: es.enter_context(nc.semaphore("s_" + e)) for e in ENGS if e != "sp"}
            dsem = {}
            for i, k in enumerate(self.dma_cnt):
                dsem[k] = es.enter_context(nc.semaphore("d%d" % i))
            for e in ENGS:
                waited = {}
                for op in self.eng_ops[e]:
                    for d in op.deps:
                        if d.dma is not None:
                            sem = dsem[d.dma]
                            skey = ("d", d.dma)
                        else:
                            sem = esem[d.eng]
                            skey = ("e", d.eng)
                        if waited.get(skey, 0) >= d.sig:
                            continue
                        waited[skey] = d.sig
                        op.waits.append((sem, d.sig))
            block = es.enter_context(nc.Block())
            engmap = {"pe": block.tensor, "act": block.scalar, "dve": block.vector,
                      "pool": block.gpsimd, "sp": block.sync}

            def make(e):
                ops = self.eng_ops[e]
                fin = e == "sp"

                def body(eng):
                    for op in ops:
                        nw = len(op.waits)
                        na = 0 if e == "pe" else WAIT_ATTACH
                        for sem, v in op.waits[:max(0, nw - na)]:
                            eng.wait_ge(sem, v)
                        inst = op.emit(eng)
                        for sem, v in op.waits[max(0, nw - na):]:
                            inst._wait_ge(sem, v)
                        if self.tagmap is not None:
                            self.tagmap[inst.ins.name] = op.tag
                        if op.dma is not None:
                            inst.then_inc(dsem[op.dma], 16)
                        elif op.needs_sig:
                            inst.then_inc(esem[e], 1)
                    if fin:
                        for k in self.final_keys:
                            eng.wait_ge(dsem[k], self.dma_cnt[k])
                return body

            for e in ENGS:
                if self.eng_ops[e] or e == "sp":
                    engmap[e](make(e))

    def stats(self):
        return {e: len(self.eng_ops[e]) for e in ENGS}

    def check_no_deadlock(self):
        ptr = {e: 0 for e in ENGS}
        done = set()
        total = sum(len(v) for v in self.eng_ops.values())
        ndone = 0
        progress = True
        while progress:
            progress = False
            for e in ENGS:
                ops = self.eng_ops[e]
                while ptr[e] < len(ops):
                    op = ops[ptr[e]]
                    if all(id(d) in done for d in op.deps):
                        done.add(id(op))
                        ptr[e] += 1
                        ndone += 1
                        progress = True
                    else:
                        break
        if ndone != total:
            stuck = {e: (ptr[e], len(self.eng_ops[e]), self.eng_ops[e][ptr[e]].tag if ptr[e] < len(self.eng_ops[e]) else None) for e in ENGS}
            raise RuntimeError("schedule deadlock: %r" % (stuck,))


class WStream:
    def __init__(self, builder, seq, dry):
        self.b = builder
        self.seq = seq
        self.dry = dry
        self.pos = 0
        self.loaded = 0
        self.nopen = 0
        self.free = list(range(NSLOT))
        self.slot_of = {}

    def _pump(self):
        while self.loaded < len(self.seq) and self.free:
            m = self.loaded
            if self.seq[m] not in self.b.converted:
                break
            slot = self.free.pop(0)
            self.slot_of[m] = slot
            self.loaded += 1
            self.b.load_piece(self.seq[m], slot)
            assert self.loaded == m + 1, "re-entrant weight pump"

    def preload(self, m):
        assert self.loaded == m and self.pos <= m
        slot = self.free.pop(0)
        self.slot_of[m] = slot
        self.loaded += 1
        return slot

    def get(self, piece):
        self.nopen += 1
        if self.dry:
            n = len(self.seq)
            self.seq.append(piece)
            if n in self.slot_of:
                return (n, self.slot_of.pop(n))
            return (-1, 0)
        n = self.pos
        assert self.seq[n] == piece, (n, self.seq[n], piece)
        self.pos += 1
        self._pump()
        assert self.loaded > n, "weight slot deadlock"
        return (n, self.slot_of.pop(n))

    def done(self, h):
        self.nopen -= 1
        if self.dry:
            if h[0] >= 0:
                self.free.append(h[1])
            return
        self.free.append(h[1])
        self._pump()


class TileDesc:
    pass


def make_tiles():
    tiles = []
    for s in range(NSEQ):
        for i in range(SEQ // T):
            td = TileDesc()
            td.kind = "p"
            td.T = T
            td.nblk = T // 128
            td.bp = 128
            td.tok0 = s * SEQ + i * T
            td.seq = s
            td.first = i == 0
            td.last = i == SEQ // T - 1
            td.L = 64
            td.segs = [0]
            td.nch = T // 64
            tiles.append(td)
    td = TileDesc()
    td.kind = "s"
    td.T = NSS * SL
    td.nblk = 1
    td.bp = NSS * SL
    td.tok0 = 0
    td.seq = 0
    td.first = False
    td.last = True
    td.L = SL
    td.segs = list(range(NSS))
    td.nch = 1
    tiles.append(td)
    for i, td in enumerate(tiles):
        td.idx = i
        td.s = i % 2
    return tiles


P_WIN, P_WOUT, P_WUP, P_WDN, P_WG, P_WPLE = 0, 4, 6, 14, 22, 24
NPIECE = 25
C_Q, C_K, C_V, C_G, C_A = 0, 512, 640, 768, 1280


def wcol(col):
    return P_WIN + col // 512, col % 512


class Builder:
    def __init__(self, nc, dram, sb, ps, P, wseq, dry):
        self.nc = nc
        self.d = dram
        self.sb = sb
        self.ps = ps
        self.psb = [p.bitcast(BF16) for p in ps]
        self.P = P
        self.dry = dry
        self.W = WStream(self, wseq, dry)
        self.fb = 0
        self.bb = 0
        self.ub = 0
        self.c_active = False
        self.no_pull = False
        self.early_load = None
        self.rr = 0
        self.hslot = {}
        self.round_cost = {}
        self.cost_acc = {}
        self.cur_round = 0
        self.collect = False
        self.filler_q = []
        self.in_filler = False
        self.credit = 0.0
        self.ratio = 1.0

    def add(self, eng, emit, reads=(), writes=(), **k):
        if not self.in_filler and not self.collect and not self.P.dry and self.filler_q and not self.no_pull:
            t = self.P.ready_time(reads, writes, eng) + SYNC_LAT
            if eng == "pe" or t - self.P.eng_free["pe"] > BLOCK_NS:
                self.pull_until(t)
        return self.P.add(eng, emit, reads, writes, **k)

    def fbank(self):
        b = self.fb % 4
        self.fb += 1
        return b

    def bbank(self):
        b = 6 + self.bb % 2
        self.bb += 1
        return b

    def ubank(self):
        if self.c_active:
            b = 4 + self.ub % 2
        else:
            b = 4 + self.ub % 4
        self.ub += 1
        return b

    def pe_cost(self, cost):
        cost = max(cost, 64)
        k = (self.cur_round, self.in_filler)
        self.cost_acc[k] = self.cost_acc.get(k, 0) + cost

    def pull_one(self):
        q = self.filler_q
        f, c, _np, _dn = q.pop(0)
        tag = self.P.cur_tag
        self.P.cur_tag = "B%d.g" % (self.cur_round - 1)
        self.in_filler = True
        f()
        self.in_filler = False
        self.P.cur_tag = tag

    def pull_filler(self, everything=False):
        while self.filler_q:
            self.pull_one()

    def pull_until(self, t_ready):
        if self.in_filler or self.collect:
            return
        P = self.P
        q = self.filler_q
        while q and P.eng_free["pe"] + FILL_MARGIN < t_ready:
            if q[0][2] and self.W.nopen >= NSLOT - 2:
                break
            if q[0][3] and self.c_active:
                break
            self.pull_one()

    def pace(self):
        if self.in_filler or self.collect or not self.filler_q:
            return
        k = (self.cur_round, False)
        done = self.cost_acc.get(k, 0) - self.round_base
        tot = self.round_cost.get(k, 0)
        if tot <= 0:
            return
        want = int(self.n_filler0 * min(1.0, PACE * done / tot))
        q = self.filler_q
        while q and (self.n_filler0 - len(q)) < want:
            if q[0][2] and self.W.nopen >= NSLOT - 2:
                break
            if q[0][3] and self.c_active:
                break
            self.pull_one()

    def n_of(self, ap):
        n = 1
        for v in ap.shape[1:]:
            n *= v
        return n

    def mm(self, out, lhsT, rhs, start, stop, reads, writes, tp=None):
        cost = self.n_of(rhs)
        if tp is not None:
            cost = cost // 3 if lhsT.shape[0] <= 32 else cost // 2
        cost = max(cost, 64)
        dur = cost / PE_GHZ + 8.0
        if tp is None:
            self.add("pe", lambda e: e.matmul(out, lhsT=lhsT, rhs=rhs, start=start, stop=stop), reads, writes, dur=dur)
        else:
            self.add("pe", lambda e: e.matmul(out, lhsT=lhsT, rhs=rhs, start=start, stop=stop, tile_position=tp), reads, writes, dur=dur)
        self.pe_cost(cost)
        if stop:
            self.pace()

    def tr(self, out, in_, ident, reads, writes):
        self.add("pe", lambda e: e.transpose(out=out, in_=in_, identity=ident), reads, writes, dur=128 / PE_GHZ + 8.0)
        self.pe_cost(128)

    def act(self, out, in_, func, reads, writes, scale=1.0, bias=0.0, accum=None):
        dur = (self.n_of(out) + 224) / 1.2
        if not isinstance(scale, float) or not isinstance(bias, float):
            dur += 90.0
        if func == AF.Sqrt or func == AF.Exp or func == AF.Tanh:
            dur += 300.0
        if accum is None:
            self.add("act", lambda e: e.activation(out=out, in_=in_, func=func, bias=bias, scale=scale), reads, writes, dur=dur)
        else:
            self.add("act", lambda e: e.activation(out=out, in_=in_, func=func, bias=bias, scale=scale, accum_out=accum), reads, writes, dur=dur + 90.0)

    def vdur(self, eng, out, psum=False):
        n = self.n_of(out)
        if eng == "pool":
            return 100.0 + 2.0 * n
        return (n + (120 if psum else 60)) / 0.96

    def ts(self, eng, out, in0, s1, s2, op0, op1, reads, writes):
        dur = self.vdur(eng, out)
        if s2 is None:
            self.add(eng, lambda e: e.tensor_scalar(out=out, in0=in0, scalar1=s1, scalar2=None, op0=op0), reads, writes, dur=dur)
        else:
            self.add(eng, lambda e: e.tensor_scalar(out=out, in0=in0, scalar1=s1, scalar2=s2, op0=op0, op1=op1), reads, writes, dur=dur)

    def tt(self, eng, out, in0, in1, op, reads, writes):
        self.add(eng, lambda e: e.tensor_tensor(out=out, in0=in0, in1=in1, op=op), reads, writes, dur=self.vdur(eng, out, True))

    def stt(self, eng, out, in0, scalar, in1, op0, op1, reads, writes):
        self.add(eng, lambda e: e.scalar_tensor_tensor(out=out, in0=in0, scalar=scalar, in1=in1, op0=op0, op1=op1), reads, writes,
                 dur=self.vdur(eng, out, True))

    def cp(self, eng, out, in_, reads, writes):
        self.add(eng, lambda e: e.tensor_copy(out=out, in_=in_), reads, writes, dur=self.vdur(eng, out, True))

    def recip(self, out, in_, reads, writes):
        self.add("dve", lambda e: e.reciprocal(out=out, in_=in_), reads, writes, dur=80.0 + 3.0 * self.n_of(out))

    def dma(self, out, in_, reads, writes, key, final=False, slow=False, timed=True):
        dur = None
        if timed:
            nbytes = self.n_of(out) * out.shape[0] * (4 if out.dtype == F32 else 2)
            dur = 2000.0 + nbytes / 150.0
        if slow:
            self.add("sp", lambda e: e.dma_start(out=out, in_=in_, allow_slow_non_contiguous=True), reads, writes, dma=key, final=final, dur=dur)
        else:
            self.add("sp", lambda e: e.dma_start(out=out, in_=in_), reads, writes, dma=key, final=final, dur=dur)

    def wkeys(self, slot):
        return [("w", slot, kc) for kc in range(8)]

    def load_piece(self, piece, slot):
        sb = self.sb
        self.no_pull = True
        self._load_piece(piece, slot)
        self.no_pull = False

    def _load_piece(self, piece, slot):
        sb = self.sb
        self.dma(sb["wsl"][:, slot].rearrange("p k n -> p (k n)"), self.d["wscr"][piece],
                 [("wscr", piece)], self.wkeys(slot), ("wld", slot), timed=False)

    def prologue(self):
        sb, d, ps = self.sb, self.d, self.ps
        identf, ident, ones = sb["identf"], sb["ident"], sb["ones"]
        self.add("pool", lambda e: e.memset(identf[:], 0.0), [], ["identf"])
        self.add("pool", lambda e: e.affine_select(out=identf[:], in_=identf[:], pattern=[[-1, 128]],
                                                   compare_op=ALU.not_equal, fill=1.0, base=0, channel_multiplier=1),
                 ["identf"], ["identf"])
        self.cp("dve", ident[:], identf[:], ["identf"], ["ident"])
        self.add("dve", lambda e: e.memset(ones[:], 1.0), [], ["ones"])
        self.dma(sb["gvin"][0:4, :], d["gvec"], [], [("tgate", 0), ("tgate", 1)], "c_gv")
        self.dma(sb["cvin"][0:34, :], d["cvec"], [], ["mu"], "c_cv")
        self.dma(sb["gfin"][:], d["gfin"].partition_broadcast(128), [], ["gfin"], "c_gf")
        self.dma(sb["es"][0:64, :], d["sinks"][0:4].partition_broadcast(64), [], ["es0"], "c_s0")
        self.dma(sb["es"][64:128, :], d["sinks"][4:8].partition_broadcast(64), [], ["es1"], "c_s1")
        if self.early_load is not None:
            self.early_load()
        self.act(sb["es"][:], sb["es"][:], AF.Exp, ["es0", "es1"], ["es"])
        bk = self.fbank()
        for kc in range(8):
            self.tr(ps[bk][:, kc * 4:(kc + 1) * 4], sb["gvin"][0:4, kc * 128:(kc + 1) * 128], identf[0:4, 0:4],
                    [("tgate", 0), ("tgate", 1), "identf"], [("ps", bk)])
        self.cp("dve", sb["gw"][:].rearrange("p k s -> p (k s)"), ps[bk][:, 0:32], [("ps", bk)], ["gw"])
        bk = self.fbank()
        for c in range(4):
            self.tr(ps[bk][:, c * 34:(c + 1) * 34], sb["cvin"][0:34, c * 128:(c + 1) * 128], identf[0:34, 0:34],
                    ["mu", "identf"], [("ps", bk)])
        self.cp("dve", sb["cw"][:].rearrange("p c s -> p (c s)"), ps[bk][:, 0:136], [("ps", bk)], ["cw"])
        self.ts("dve", sb["cw"][:, :, 0:31], sb["cw"][:, :, 0:31], 0.5, None, ALU.mult, None, ["cw"], ["cw"])
        self.ts("dve", sb["cw2"][:], sb["cw"][:, :, 32:34], 0.5, None, ALU.mult, None, ["cw"], ["cw2"])
        m32 = sb["mask32"]
        self.tt("dve", m32[:], identf[:, 0:32], identf[:, 32:64], ALU.add, ["identf"], ["mask32"])
        self.tt("dve", m32[:], m32[:], identf[:, 64:96], ALU.add, ["identf", "mask32"], ["mask32"])
        self.tt("dve", m32[:], m32[:], identf[:, 96:128], ALU.add, ["identf", "mask32"], ["mask32"])
        for c in range(4):
            self.tt("dve", sb["dg"][:, c], m32[:].unsqueeze(1).to_broadcast([128, 31, 32]),
                    sb["cw"][:, c, 0:31].unsqueeze(2).to_broadcast([128, 31, 32]), ALU.mult, ["mask32", "cw"], ["dg"])
        self.converted = set()
        self.cvn = 0
        self.cv_pending = None
        self.cv_pipe(0, pre_m=0)
        self.cv_pipe(1, pre_m=1)
        head = [(lambda piece=piece: self.cv_pipe(piece, pre_m=piece if piece < NPRE else None)) for piece in range(2, P_WUP)]
        rest = [(lambda piece=piece: self.cv_pipe(piece)) for piece in list(range(P_WUP, P_WPLE)) + [P_WPLE]]
        rest.append(lambda: self.cv_pipe(None))
        return head, rest

    def convert_piece(self, piece, pre_m=None):
        self.cv_finish(self.cv_load(piece, pre_m))

    def cv_pipe(self, piece, pre_m=None):
        ctx = self.cv_load(piece, pre_m) if piece is not None else None
        if self.cv_pending is not None:
            self.cv_finish(self.cv_pending)
        self.cv_pending = ctx

    def cv_load(self, piece, pre_m=None):
        sb, d = self.sb, self.d
        stg = sb["hT"].bitcast(F32).reshape([128, 2, 8, 512])
        half = self.cvn % 2
        self.cvn += 1
        hk = [("hT", c) for c in range(16 * half, 16 * half + 16)]
        nkc, ncols, gset = 8, 512, None
        if piece < P_WOUT:
            cb = (piece - P_WIN) * 512
            ncols = min(512, 1792 - cb)
            src = d["w_in"][:, cb:cb + ncols].rearrange("(kc p) n -> p kc n", p=128)
            gset = 0
        elif piece < P_WUP:
            hf = piece - P_WOUT
            src = d["w_out"][:, hf * 512:(hf + 1) * 512].rearrange("(kc p) n -> p kc n", p=128)
            gset = 1
        elif piece < P_WDN:
            j = piece - P_WUP
            src = d["w_up"][:, j * 512:(j + 1) * 512].rearrange("(kc p) n -> p kc n", p=128)
            gset = 2
        elif piece < P_WG:
            q = piece - P_WDN
            hf, kg = q // 4, q % 4
            src = d["w_down"][kg * 1024:(kg + 1) * 1024, hf * 512:(hf + 1) * 512].rearrange("(kc p) n -> p kc n", p=128)
        elif piece < P_WPLE:
            hf = piece - P_WG
            src = d["w_gate"][:, hf * 512:(hf + 1) * 512].rearrange("(kc p) n -> p kc n", p=128)
            gset = 3
        else:
            nkc = 4
            src = d["w_ple"].rearrange("(kc p) (h n) -> p kc h n", p=128, h=2)
        if piece == P_WPLE:
            for kc in range(2):
                self.dma(stg[:, half, 2 * kc:2 * kc + 2, :], src[:, kc], [], hk, ("wfld", half))
        else:
            self.dma(stg[:, half, 0:nkc, 0:ncols], src, [], hk, ("wfld", half))
        return (piece, pre_m, stg, half, hk, nkc, ncols, gset)

    def cv_finish(self, ctx):
        sb, d = self.sb, self.d
        piece, pre_m, stg, half, hk, nkc, ncols, gset = ctx
        pslot = None
        if pre_m is not None:
            pslot = self.W.preload(pre_m)
        dstt = sb["wple"] if piece == P_WPLE else (sb["actC"] if pslot is None else sb["wsl"][:, pslot])
        if pslot is not None and ncols < 512:
            tail = sb["wsl"][:, pslot, :, ncols:512]
            self.add("pool", lambda e: e.memset(tail, 0.0), [], [("wtail", pslot)])
        for kc in range(nkc):
            eng = "act" if (kc % 2 == 0) else "dve"
            o = dstt[:, kc, 0:ncols]
            i_ = stg[:, half, kc, 0:ncols]
            wk = [("wple", kc)] if piece == P_WPLE else ([("actC", kc, b_) for b_ in range(4)] if pslot is None else [("w", pslot, kc)])
            if POOL_CONVERT and piece >= 2:
                if gset is None:
                    self.cp("pool", o, i_, hk, wk)
                else:
                    g = sb["gw"][:, kc, gset:gset + 1].to_broadcast([128, ncols])
                    self.tt("pool", o, i_, g, ALU.mult, hk + ["gw"], wk)
                continue
            if gset is None:
                if eng == "act":
                    self.act(o, i_, AF.Copy, hk, wk)
                else:
                    self.cp(eng, o, i_, hk, wk)
            else:
                g = sb["gw"][:, kc, gset:gset + 1]
                if eng == "act":
                    self.act(o, i_, AF.Copy, hk + ["gw"], wk, scale=g)
                else:
                    self.ts(eng, o, i_, g, None, ALU.mult, None, hk + ["gw"], wk)
        if piece != P_WPLE:
            if pslot is None:
                rk = [("actC", kc, b_) for kc in range(8) for b_ in range(4)]
                self.dma(d["wscr"][piece], sb["actC"][:].rearrange("p k n -> p (k n)"), rk, [("wscr", piece)], "wst")
            else:
                self.dma(d["wscr"][piece], sb["wsl"][:, pslot].rearrange("p k n -> p (k n)"),
                         self.wkeys(pslot) + [("wtail", pslot)], [("wscr", piece)], ("wst", pslot))
            self.converted.add(piece)
            if not self.dry:
                self.W._pump()

    def tm_norm_stat_block(self, td, site, b, junk, jkey):
        sb = self.sb
        s, bp = td.s, td.bp
        x = sb["x"][s]
        ssq, rs = sb["ssq"][site], sb["rs"][site]
        self.act(junk, x[0:bp, b, :], AF.Square, [("x", s, b)], [jkey, ("ssq", site, b)], accum=ssq[0:bp, b:b + 1])
        self.act(rs[0:bp, b:b + 1], ssq[0:bp, b:b + 1], AF.Sqrt, [("ssq", site, b)], [("rs", site, b)], scale=1.0 / D, bias=EPS)
        self.recip(rs[0:bp, b:b + 1], rs[0:bp, b:b + 1], [("rs", site, b)], [("rs", site, b)])

    def tm_norm_A(self, td, site, b):
        sb = self.sb
        s, bp = td.s, td.bp
        x = sb["x"][s]
        hs = self.rr % 2
        self.rr += 1
        self.hslot[(td.idx, site, b)] = hs
        hst = sb["hst"][hs]
        self.tm_norm_stat_block(td, site, b, hst[0:bp, :], ("hst", hs))
        self.ts("dve", hst[0:bp, :], x[0:bp, b, :], sb["rs"][site][0:bp, b:b + 1], None, ALU.mult, None,
                [("x", s, b), ("rs", site, b)], [("hst", hs)])

    def tm_norm_B(self, td, site, b, dst, dkey, bank, evac_eng):
        sb = self.sb
        bp = td.bp
        hs = self.hslot.pop((td.idx, site, b))
        hst = sb["hst"][hs]
        pb = self.psb[bank]
        for kc in range(8):
            self.tr(pb[:, kc * bp:(kc + 1) * bp], hst[0:bp, kc * 128:(kc + 1) * 128], sb["ident"][0:bp, 0:bp],
                    [("hst", hs), "ident"], [("ps", bank)])
        o = dst[:, :, b * 128:b * 128 + bp]
        i_ = pb[:, 0:8 * bp].rearrange("p (k t) -> p k t", k=8)
        wk = [(dkey, kc, b) for kc in range(8)]
        if evac_eng == "act":
            self.act(o, i_, AF.Copy, [("ps", bank)], wk)
        else:
            self.cp(evac_eng, o, i_, [("ps", bank)], wk)

    def norm_pipeline(self, td, site, dst, dkey, bankfn, evac_eng, pre=None):
        nb = td.nblk
        steps = []
        for k in range(nb + 1):
            def st(k=k):
                if pre is not None and k < len(pre) and pre[k] is not None:
                    pre[k]()
                if k < nb:
                    self.tm_norm_A(td, site, k)
                if k >= 1:
                    self.tm_norm_B(td, site, k - 1, dst, dkey, bankfn(), evac_eng)
            steps.append(st)
        return steps

    def front_steps(self, td):
        sb, d, ps = self.sb, self.d, self.ps
        s, Tt, nb, bp, L = td.s, td.T, td.nblk, td.bp, td.L
        x = sb["x"][s]
        aT = sb["actT"]
        aK = "actT"
        steps = []
        xsrc = d["xp"] if td.kind == "p" else d["xs"]
        psrc = d["pp"] if td.kind == "p" else d["psm"]

        def f_load():
            for b in range(nb):
                self.dma(x[0:bp, b, :], xsrc[td.tok0 + b * 128:td.tok0 + b * 128 + bp, :], [], [("x", s, b)], ("xld", s, b))
            if td.kind == "p" and td.first:
                self.add("pool", lambda e: e.memset(sb["u2"][:, :, 0:30], 0.0), [], [("u2", "halo")])
            if td.kind == "s":
                for sg in td.segs:
                    self.dma(sb["var"][0:30, :], d["sc"][sg], [], ["var"], "scld")
                    bk = self.fbank()
                    for c in range(4):
                        self.tr(ps[bk][:, c * 30:(c + 1) * 30], sb["var"][0:30, c * 128:(c + 1) * 128],
                                sb["identf"][0:30, 0:30], ["var", "identf"], [("ps", bk)])
                    self.act(sb["u2"][:, :, sg * 46:sg * 46 + 30], ps[bk][:, 0:120].rearrange("p (c t) -> p c t", c=4),
                             AF.Copy, [("ps", bk)], [("u2", "halo", sg), ("u2", "halo")] + [("u2", c) for c in range(4)], scale=2.0)
                    self.act(sb["utail"][:, sg, :, 0:14], ps[bk][:, 0:120].rearrange("p (c t) -> p c t", c=4)[:, :, 16:30],
                             AF.Copy, [("ps", bk)], [("utail", sg, "a")], scale=2.0)
        steps.append(f_load)

        steps.extend(self.norm_pipeline(td, 0, aT, aK, self.fbank, "act"))

        def ak(kc):
            return [(aK, kc, b_) for b_ in range(nb)]

        def u2cols(sg):
            return 30 if td.kind == "p" else sg * 46 + 30

        def f_q():
            h = self.W.get(P_WIN + 0)
            wsl = sb["wsl"][:, h[1]]
            for G in range(4):
                bk = self.fbank()
                for kc in range(8):
                    self.mm(ps[bk][:, 0:Tt], wsl[:, kc, G * 128:(G + 1) * 128], aT[:, kc, 0:Tt], kc == 0, kc == 7,
                            [("w", h[1], kc)] + ak(kc), [("ps", bk)])
                self.act(sb["qT"][:, G, 0:Tt], ps[bk][:, 0:Tt], AF.Copy, [("ps", bk)], [("qT", G)])
            self.W.done(h)
        steps.append(f_q)

        def g_chunk(h, c):
            piece, off = wcol(C_G + c * 128)
            wsl = sb["wsl"][:, h[1]]
            bk = self.fbank()
            for kc in range(8):
                self.mm(ps[bk][:, 0:Tt], wsl[:, kc, off:off + 128], aT[:, kc, 0:Tt], kc == 0, kc == 7,
                        [("w", h[1], kc)] + ak(kc), [("ps", bk)])
            self.act(sb["tg"][:, c, 0:Tt], ps[bk][:, 0:Tt], AF.Tanh, [("ps", bk)], [("tg", c)], scale=0.5)

        def a_chunk(h, c):
            piece, off = wcol(C_A + c * 128)
            wsl = sb["wsl"][:, h[1]]
            bk = self.fbank()
            for kc in range(8):
                self.mm(ps[bk][:, 0:Tt], wsl[:, kc, off:off + 128], aT[:, kc, 0:Tt], kc == 0, kc == 7,
                        [("w", h[1], kc)] + ak(kc), [("ps", bk)])
            for sg in td.segs:
                t0 = 0 if td.kind == "p" else sg * L
                tl = Tt if td.kind == "p" else L
                uc = u2cols(sg)
                self.stt("dve", sb["u2"][:, c, uc:uc + tl], sb["tg"][:, c, t0:t0 + tl], 1.0, ps[bk][:, t0:t0 + tl],
                         ALU.add, ALU.mult, [("tg", c), ("ps", bk)], [("u2", c)])
                if td.last:
                    if td.kind == "p":
                        self.stt("dve", sb["utail"][:, 0, c, :], sb["tg"][:, c, Tt - 30:Tt], 1.0, ps[bk][:, Tt - 30:Tt],
                                 ALU.add, ALU.mult, [("tg", c), ("ps", bk)], [("utail", 0, c)])
                    else:
                        self.stt("dve", sb["utail"][:, sg, c, 14:30], sb["tg"][:, c, t0:t0 + tl], 1.0, ps[bk][:, t0:t0 + tl],
                                 ALU.add, ALU.mult, [("tg", c), ("ps", bk)], [("utail", sg, c)])

        def f_kvg():
            h = self.W.get(P_WIN + 1)
            wsl = sb["wsl"][:, h[1]]
            wr = [("w", h[1], kc) for kc in range(8)]
            bk = self.fbank()
            for kc in range(8):
                self.mm(ps[bk][:, 0:Tt], wsl[:, kc, 0:128], aT[:, kc, 0:Tt], kc == 0, kc == 7,
                        [("w", h[1], kc)] + ak(kc), [("ps", bk)])
            self.act(sb["kT"][:, 128:128 + Tt], ps[bk][:, 0:Tt], AF.Copy, [("ps", bk)], [("kT", "body")])
            bk = self.fbank()
            nchunks = td.nch * len(td.segs)
            for ci in range(nchunks):
                t0 = ci * L
                for hh in range(2):
                    for kc in range(8):
                        self.mm(ps[bk][hh * 64:hh * 64 + L, ci * 64:(ci + 1) * 64], aT[:, kc, t0:t0 + L],
                                wsl[:, kc, 128 + hh * 64:128 + (hh + 1) * 64], kc == 0, kc == 7,
                                [("w", h[1], kc)] + ak(kc), [("ps", bk)], tp=(0, hh * 64))
            if td.kind == "p":
                self.cp("dve", sb["V"][:, 2:2 + nchunks, :], ps[bk][:, 0:nchunks * 64].rearrange("p (c d) -> p c d", d=64),
                        [("ps", bk)], [("V", "body")])
            else:
                for hh in range(2):
                    self.cp("dve", sb["V"][hh * 64:hh * 64 + L, 2:2 + nchunks, :],
                            ps[bk][hh * 64:hh * 64 + L, 0:nchunks * 64].rearrange("p (c d) -> p c d", d=64),
                            [("ps", bk)], [("V", "body")])
            if td.last:
                bk = self.fbank()
                m0 = Tt - 128 if td.kind == "p" else 0
                mrows = 128 if td.kind == "p" else Tt
                for kc in range(8):
                    self.mm(ps[bk][0:mrows, 0:256], aT[:, kc, m0:m0 + mrows], wsl[:, kc, 0:256], kc == 0, kc == 7,
                            [("w", h[1], kc)] + ak(kc), [("ps", bk)])
                self.act(sb["kvst"][0:mrows, :], ps[bk][0:mrows, 0:256], AF.Copy, [("ps", bk)], ["kvst"])
                if td.kind == "p":
                    self.dma(d["nkp"][td.seq], sb["kvst"][:, 0:128], ["kvst"], [], "kvst", final=True)
                    self.dma(d["nvp"][td.seq], sb["kvst"][:, 128:256], ["kvst"], [], "kvst", final=True)
                else:
                    for sg in td.segs:
                        self.dma(d["nks"][sg], sb["kvst"][sg * L:(sg + 1) * L, 0:128], ["kvst"], [], "kvst", final=True)
                        self.dma(d["nvs"][sg], sb["kvst"][sg * L:(sg + 1) * L, 128:256], ["kvst"], [], "kvst", final=True)
            g_chunk(h, 0)
            g_chunk(h, 1)
            self.W.done(h)
        steps.append(f_kvg)

        def f_ga():
            h = self.W.get(P_WIN + 2)
            g_chunk(h, 2)
            g_chunk(h, 3)
            a_chunk(h, 0)
            a_chunk(h, 1)
            self.W.done(h)
        steps.append(f_ga)

        def f_a2():
            h = self.W.get(P_WIN + 3)
            a_chunk(h, 2)
            a_chunk(h, 3)
            self.W.done(h)
        steps.append(f_a2)

        chunks = []
        if td.kind == "p":
            for c in range(td.nch):
                ents = []
                for kj in (c - 2, c - 1, c):
                    if kj < 0:
                        if td.first:
                            continue
                        ents.append(((kj + 2) * 64, 64, kj + 2, ("kT", "halo"), ("V", "halo")))
                    else:
                        ents.append((128 + kj * 64, 64, 2 + kj, ("kT", "body"), ("V", "body")))
                chunks.append((0, c * 64, ents))
        else:
            for sg in td.segs:
                ents = [(0, 64, 0, ("kT", "halo"), ("V", "halo")), (64, 64, 1, ("kT", "halo"), ("V", "halo")),
                        (128 + sg * L, L, 2 + sg, ("kT", "body"), ("V", "body"))]
                chunks.append((sg, sg * L, ents))
        N4 = 4 * L
        qk = [("qT", G) for G in range(4)]

        def halo_from_cache(sg):
            self.dma(sb["ckst"][:], d["ck"][sg], [], ["ckst"], "ckld")
            bk = self.fbank()
            self.tr(ps[bk][:, 0:128], sb["ckst"][:], sb["identf"][:], ["ckst", "identf"], [("ps", bk)])
            self.act(sb["kT"][:, 0:128], ps[bk][:, 0:128], AF.Copy, [("ps", bk)], [("kT", "halo")])
            for hh in range(2):
                for c2 in range(2):
                    self.dma(sb["cvst"][hh * 64:(hh + 1) * 64, c2, :], d["cv"][sg, c2 * 64:(c2 + 1) * 64, hh * 64:(hh + 1) * 64],
                             [], ["cvst"], "cvld")
            self.cp("dve", sb["V"][:, 0:2, :], sb["cvst"][:], ["cvst"], [("V", "halo")])

        def att_scores(ci):
            sg, q0, ents = chunks[ci]
            par = ci % 2
            bx, by = 2 * par, 2 * par + 1
            for i, (kc0, nk, vs, kkey, vkey) in enumerate(ents):
                bk = bx if i < 2 else by
                co = (i % 2) * 256
                for hh in range(2):
                    self.mm(ps[bk][hh * 64:hh * 64 + nk, co:co + N4].rearrange("p (g q) -> p g q", g=4),
                            sb["kT"][hh * 64:(hh + 1) * 64, kc0:kc0 + nk],
                            sb["qT"][hh * 64:(hh + 1) * 64, :, q0:q0 + L], True, True,
                            [kkey] + qk, [("ps", bk)], tp=(hh * 64, hh * 64))
            PT = sb["PT"][par]
            for i, (kc0, nk, vs, kkey, vkey) in enumerate(ents):
                bk = bx if i < 2 else by
                co = (i % 2) * 256
                if nk == 64:
                    self.act(PT[:, i, 0:N4], ps[bk][:, co:co + N4], AF.Exp, [("ps", bk)], [("PT", par, i)], scale=0.125)
                else:
                    for hh in range(2):
                        self.act(PT[hh * 64:hh * 64 + nk, i, 0:N4], ps[bk][hh * 64:hh * 64 + nk, co:co + N4], AF.Exp,
                                 [("ps", bk)], [("PT", par, i)], scale=0.125)

        def att_pv(ci):
            sg, q0, ents = chunks[ci]
            par = ci % 2
            bx, by = 2 * par, 2 * par + 1
            PT = sb["PT"][par]
            ne = len(ents)
            for i, (kc0, nk, vs, kkey, vkey) in enumerate(ents):
                for hh in range(2):
                    self.mm(ps[bx][hh * 64:(hh + 1) * 64, 0:N4], sb["V"][hh * 64:hh * 64 + nk, vs, :],
                            PT[hh * 64:hh * 64 + nk, i, 0:N4], i == 0, i == ne - 1,
                            [vkey, ("PT", par, i)], [("ps", bx)], tp=(hh * 64, hh * 64))
            for i, (kc0, nk, vs, kkey, vkey) in enumerate(ents):
                for hh in range(2):
                    self.mm(ps[by][hh * 64:(hh + 1) * 64, 256:256 + N4], sb["ones"][hh * 64:hh * 64 + nk, 0:64],
                            PT[hh * 64:hh * 64 + nk, i, 0:N4], i == 0, i == ne - 1,
                            ["ones", ("PT", par, i)], [("ps", by)], tp=(hh * 64, hh * 64))
            rden = sb["rden"]
            self.tt("dve", rden[:, :, 0:L], ps[by][:, 256:256 + N4].rearrange("p (g q) -> p g q", g=4),
                    sb["es"][:].unsqueeze(2).to_broadcast([128, 4, L]), ALU.add, [("ps", by), "es"], ["rden"])
            self.recip(rden[:, :, 0:L], rden[:, :, 0:L], ["rden"], ["rden"])
            self.tt("dve", sb["attnT"][:, :, q0:q0 + L], ps[bx][:, 0:N4].rearrange("p (g q) -> p g q", g=4),
                    rden[:, :, 0:L], ALU.mult, [("ps", bx), "rden"], [("attnT", q0 // 128)])

        if td.kind == "p":
            def f_att0():
                self.fb = 0
                att_scores(0)
            steps.append(f_att0)
            for ci in range(len(chunks)):
                def f_att(ci=ci):
                    if ci + 1 < len(chunks):
                        att_scores(ci + 1)
                    att_pv(ci)
                steps.append(f_att)
        else:
            for ci in range(len(chunks)):
                def f_atts(ci=ci):
                    halo_from_cache(chunks[ci][0])
                    self.fb = 0
                    att_scores(ci)
                    att_pv(ci)
                steps.append(f_atts)

        attn_keys = [("attnT", j) for j in range((Tt + 127) // 128)]

        def fm_rms(src, skeys_fn, eps):
            bk = self.fbank()
            for c in range(4):
                sl = self.rr % 2
                self.rr += 1
                self.act(sb["sq"][:, sl, 0:Tt], src[:, c, 0:Tt], AF.Square, skeys_fn(c), [("sq", sl)])
                self.mm(ps[bk][:, 0:Tt], sb["ones"][:], sb["sq"][:, sl, 0:Tt], c == 0, c == 3,
                        ["ones", ("sq", sl)], [("ps", bk)])
            self.act(sb["sdT"][:, 0:Tt], ps[bk][:, 0:Tt], AF.Sqrt, [("ps", bk)], ["sdT"], scale=1.0 / 512, bias=eps)
            self.recip(sb["raT"][:, 0:Tt], sb["sdT"][:, 0:Tt], ["sdT"], ["raT"])
            for c in range(4):
                self.tt("dve", src[:, c, 0:Tt], src[:, c, 0:Tt], sb["raT"][:, 0:Tt], ALU.mult,
                        skeys_fn(c) + ["raT"], skeys_fn(c))

        steps.append(lambda: fm_rms(sb["attnT"], lambda c: attn_keys, EPS))

        def conv_chunk(c):
            cw, dg, u2 = sb["cw"], sb["dg"], sb["u2"]
            bk = self.fbank()
            npe = 31 - NDVE_TAPS
            for sg in td.segs:
                t0 = 0 if td.kind == "p" else sg * L
                tl = Tt if td.kind == "p" else L
                ub = 0 if td.kind == "p" else sg * 46
                ukeys = [("u2", c), ("u2", "halo"), ("u2", "halo", sg)]
                for tap in range(npe):
                    for i in range(4):
                        self.mm(ps[bk][32 * i:32 * i + 32, t0:t0 + tl], dg[32 * i:32 * i + 32, c, tap, :],
                                u2[32 * i:32 * i + 32, c, ub + tap:ub + tap + tl], tap == 0, tap == npe - 1,
                                ukeys + ["dg"], [("ps", bk)], tp=(32 * i, 32 * i))
            self.act(sb["acc"][:, c, 0:Tt], ps[bk][:, 0:Tt], AF.Identity, [("ps", bk), "cw"], [("acc", c)], bias=cw[:, c, 31:32])
            for sg in td.segs:
                t0 = 0 if td.kind == "p" else sg * L
                tl = Tt if td.kind == "p" else L
                ub = 0 if td.kind == "p" else sg * 46
                ukeys = [("u2", c), ("u2", "halo"), ("u2", "halo", sg)]
                a = sb["acc"][:, c, t0:t0 + tl]
                for tap in range(npe, 31):
                    self.stt("dve", a, u2[:, c, ub + tap:ub + tap + tl], cw[:, c, tap:tap + 1], a, ALU.mult, ALU.add,
                             ukeys + ["cw", ("acc", c)], [("acc", c)])
        for c in range(4):
            steps.append(lambda c=c: conv_chunk(c))

        def f_newconv():
            for sg in td.segs:
                bk = self.fbank()
                rk = [("utail", sg, c) for c in range(4)] + [("utail", sg, "a"), "identf"]
                for c in range(4):
                    self.tr(ps[bk][0:30, c * 128:(c + 1) * 128], sb["utail"][:, sg, c, :], sb["identf"][:], rk, [("ps", bk)])
                self.act(sb["mu"][0:30, :], ps[bk][0:30, :], AF.Copy, [("ps", bk)], ["mu"], scale=0.5)
                dst = d["ncp"][td.seq] if td.kind == "p" else d["ncs"][sg]
                self.dma(dst, sb["mu"][0:30, :], ["mu"], [], "ncst", final=True)
        if td.last:
            steps.append(f_newconv)

        def f_ln_stats():
            acc = sb["acc"]
            bk1 = self.fbank()
            bk2 = self.fbank()
            for c in range(4):
                sl = self.rr % 2
                self.rr += 1
                self.act(sb["sq"][:, sl, 0:Tt], acc[:, c, 0:Tt], AF.Copy, [("acc", c)], [("sq", sl)])
                self.mm(ps[bk1][:, 0:Tt], sb["ones"][:], sb["sq"][:, sl, 0:Tt], c == 0, c == 3,
                        ["ones", ("sq", sl)], [("ps", bk1)])
                sl = self.rr % 2
                self.rr += 1
                self.act(sb["sq"][:, sl, 0:Tt], acc[:, c, 0:Tt], AF.Square, [("acc", c)], [("sq", sl)])
                self.mm(ps[bk2][:, 0:Tt], sb["ones"][:], sb["sq"][:, sl, 0:Tt], c == 0, c == 3,
                        ["ones", ("sq", sl)], [("ps", bk2)])
            mu, var = sb["mu"], sb["var"]
            self.ts("dve", mu[:, 0:Tt], ps[bk1][:, 0:Tt], 1.0 / 512, None, ALU.mult, None, [("ps", bk1)], ["mu"])
            self.tt("dve", var[:, 0:Tt], mu[:, 0:Tt], mu[:, 0:Tt], ALU.mult, ["mu"], ["var"])
            self.stt("dve", var[:, 0:Tt], ps[bk2][:, 0:Tt], 1.0 / 512, var[:, 0:Tt], ALU.mult, ALU.subtract,
                     [("ps", bk2), "var"], ["var"])
            self.act(var[:, 0:Tt], var[:, 0:Tt], AF.Sqrt, ["var"], ["var"], bias=EPS)
            self.recip(var[:, 0:Tt], var[:, 0:Tt], ["var"], ["var"])
        steps.append(f_ln_stats)

        def ln_chunk(c):
            acc, cw, cw2 = sb["acc"], sb["cw"], sb["cw2"]
            a = acc[:, c, 0:Tt]
            self.tt("dve", a, a, sb["mu"][:, 0:Tt], ALU.subtract, [("acc", c), "mu"], [("acc", c)])
            self.stt("dve", a, a, cw[:, c, 32:33], sb["var"][:, 0:Tt], ALU.mult, ALU.mult, [("acc", c), "cw", "var"], [("acc", c)])
            sl = self.rr % 2
            self.rr += 1
            self.act(sb["sq"][:, sl, 0:Tt], a, AF.Tanh, [("acc", c), "cw2"], [("sq", sl)], scale=0.5, bias=cw2[:, c, 1:2])
            self.ts("dve", a, a, cw[:, c, 33:34], None, ALU.add, None, [("acc", c), "cw"], [("acc", c)])
            self.stt("dve", sb["cs"][:, c, 0:Tt], sb["sq"][:, sl, 0:Tt], 1.0, a, ALU.add, ALU.mult,
                     [("sq", sl), ("acc", c)], [("qT", c)])
        for c in range(4):
            steps.append(lambda c=c: ln_chunk(c))
        steps.append(lambda: fm_rms(sb["cs"], lambda c: [("qT", c)], 4.0 * EPS))

        def f_wout(b):
            if b == 0:
                self.wo_h = [self.W.get(P_WOUT + 0), self.W.get(P_WOUT + 1)]
            for hf in range(2):
                h = self.wo_h[hf]
                wsl = sb["wsl"][:, h[1]]
                bk = self.fbank()
                for kc in range(8):
                    if kc < 4:
                        lhsT = sb["attnT"][:, kc, b * 128:b * 128 + bp]
                        rk = attn_keys
                    else:
                        lhsT = sb["cs"][:, kc - 4, b * 128:b * 128 + bp]
                        rk = [("qT", kc - 4)]
                    self.mm(ps[bk][0:bp, :], lhsT, wsl[:, kc, :], kc == 0, kc == 7,
                            [("w", h[1], kc)] + rk, [("ps", bk)])
                xs = x[0:bp, b, hf * 512:(hf + 1) * 512]
                self.tt("dve", xs, xs, ps[bk][0:bp, :], ALU.add, [("x", s, b), ("ps", bk)], [("x", s, b)])
            if b == nb - 1:
                self.W.done(self.wo_h[0])
                self.W.done(self.wo_h[1])

        pre = [None] + [(lambda b=b: f_wout(b)) for b in range(1, nb)]
        steps.append(lambda: f_wout(0))
        steps.extend(self.norm_pipeline(td, 1, sb["actB"], "actB", self.fbank, "dve", pre=pre))

        def f_halo():
            self.cp("pool", sb["kT"][:, 0:128], sb["kT"][:, Tt:Tt + 128], [("kT", "body")], [("kT", "halo")])
            self.cp("pool", sb["V"][:, 0:2, :], sb["V"][:, 8:10, :], [("V", "body")], [("V", "halo")])
            self.cp("pool", sb["u2"][:, :, 0:30], sb["u2"][:, :, Tt:Tt + 30], [("u2", c) for c in range(4)], [("u2", "halo")])
        if td.kind == "p" and not td.last:
            steps.append(f_halo)
        return steps

    def back_steps(self, td):
        sb, d, ps = self.sb, self.d, self.ps
        s, Tt, nb, bp = td.s, td.T, td.nblk, td.bp
        x = sb["x"][s]
        aB = sb["actB"]
        aK = "actB"
        aC = sb["actC"]
        aCK = "actC"
        steps = []

        def ak(kc):
            return [(aK, kc, b_) for b_ in range(nb)]

        psrc = d["pp"] if td.kind == "p" else d["psm"]

        def f_pload():
            self.dma(sb["p"][s][0:bp, 0:nb, :], psrc[td.tok0:td.tok0 + Tt, :].rearrange("(b p) c -> p b c", p=bp),
                     [], [("p", s)], ("pld", s))
        steps.append(f_pload)

        groups = []

        def up_group(j, m):
            if m == 0:
                self.up_h = self.W.get(P_WUP + j)
            h = self.up_h
            wsl = sb["wsl"][:, h[1]]
            bk = self.ubank()
            for kc in range(8):
                self.mm(ps[bk][:, 0:Tt], wsl[:, kc, m * 128:(m + 1) * 128], aB[:, kc, 0:Tt], kc == 0, kc == 7,
                        [("w", h[1], kc)] + ak(kc), [("ps", bk)])
            hc = sb["hT"][:, j * 4 + m, 0:Tt]
            self.act(hc, ps[bk][:, 0:Tt], AF.Relu, [("ps", bk)], [("hT", j * 4 + m)])
            self.tt("pool", hc, hc, hc, ALU.mult, [("hT", j * 4 + m)], [("hT", j * 4 + m)])
            if m == 3:
                self.W.done(h)
        for j in range(8):
            for m in range(4):
                groups.append((lambda j=j, m=m: up_group(j, m), 8 * max(Tt, 64), m == 0, False))

        def down_group(hf, kg, q):
            if q == 0:
                self.dn_h = self.W.get(P_WDN + hf * 4 + kg)
            h = self.dn_h
            wsl = sb["wsl"][:, h[1]]
            for kc in (2 * q, 2 * q + 1):
                for b in range(nb):
                    self.mm(ps[4 + b][0:bp, :], sb["hT"][:, kg * 8 + kc, b * 128:b * 128 + bp], wsl[:, kc, :],
                            kg == 0 and kc == 0, kg == 3 and kc == 7,
                            [("w", h[1], kc), ("hT", kg * 8 + kc)], [("ps", 4 + b)])
            if q == 3:
                self.W.done(h)
                if kg == 3:
                    for b in range(nb):
                        xs = x[0:bp, b, hf * 512:(hf + 1) * 512]
                        self.tt("dve", xs, xs, ps[4 + b][0:bp, :], ALU.add, [("x", s, b), ("ps", 4 + b)], [("x", s, b)])
        for hf in range(2):
            for kg in range(4):
                for q in range(4):
                    groups.append((lambda hf=hf, kg=kg, q=q: down_group(hf, kg, q), 2 * nb * 512, q == 0, True))

        steps1 = (steps, groups)
        steps = []
        steps.extend(self.norm_pipeline(td, 2, aC, aCK, self.bbank, "dve"))

        def f_pT():
            for b in range(nb):
                self.act(sb["pbf"][0:bp, :], sb["p"][s][0:bp, b, :], AF.Copy, [("p", s)], ["pbf"])
                bk = self.bbank()
                pb = self.psb[bk]
                for k2 in range(2):
                    self.tr(pb[:, k2 * bp:(k2 + 1) * bp], sb["pbf"][0:bp, k2 * 128:(k2 + 1) * 128], sb["ident"][0:bp, 0:bp],
                            ["pbf", "ident"], [("ps", bk)])
                self.act(sb["pT"][:, :, b * 128:b * 128 + bp], pb[:, 0:2 * bp].rearrange("p (k t) -> p k t", k=2),
                         AF.Copy, [("ps", bk)], [("pT", b)])
        steps.append(f_pT)

        def f_ple(hf):
            hg = self.W.get(P_WG + hf)
            wp = sb["wple"]
            wg = sb["wsl"][:, hg[1]]
            for b in range(nb):
                bk = self.bbank()
                for kc in range(8):
                    self.mm(ps[bk][0:bp, :], aC[:, kc, b * 128:b * 128 + bp], wg[:, kc, :], kc == 0, kc == 7,
                            [("w", hg[1], kc), (aCK, kc, b)], [("ps", bk)])
                sl = self.rr % 2
                self.rr += 1
                tgt = sb["tgate"][0:bp, sl, :]
                self.act(tgt, ps[bk][0:bp, :], AF.Tanh, [("ps", bk)], [("tgate", sl)], scale=0.5)
                bk2 = self.bbank()
                for k2 in range(2):
                    self.mm(ps[bk2][0:bp, :], sb["pT"][:, k2, b * 128:b * 128 + bp], wp[:, k2 * 2 + hf, :], k2 == 0, k2 == 1,
                            [("wple", k2 * 2 + hf), ("pT", b)], [("ps", bk2)])
                self.stt("dve", tgt, tgt, 1.0, ps[bk2][0:bp, :], ALU.add, ALU.mult, [("tgate", sl), ("ps", bk2)], [("tgate", sl)])
                xs = x[0:bp, b, hf * 512:(hf + 1) * 512]
                self.stt("dve", xs, tgt, 0.5, xs, ALU.mult, ALU.add, [("tgate", sl), ("x", s, b)], [("x", s, b)])
            self.W.done(hg)
        steps.append(lambda: f_ple(0))
        steps.append(lambda: f_ple(1))

        ydst = d["yp"] if td.kind == "p" else d["ys"]

        def f_out(b):
            xb_ = x[0:bp, b, :]
            sl = self.rr % 2
            self.rr += 1
            self.tm_norm_stat_block(td, 3, b, sb["tgate"].bitcast(BF16)[0:bp, sl, :], ("tgate", sl))
            self.stt("dve", xb_, xb_, sb["rs"][3][0:bp, b:b + 1], sb["gfin"][0:bp, :], ALU.mult, ALU.mult,
                     [("x", s, b), ("rs", 3, b), "gfin"], [("x", s, b)])
            self.dma(ydst[td.tok0 + b * 128:td.tok0 + b * 128 + bp, :], xb_, [("x", s, b)], [], ("yst", s, b), final=True)
        for b in range(nb):
            steps.append(lambda b=b: f_out(b))
        return steps1, steps

    def run(self, interleave=True):
        tiles = make_tiles()
        n = len(tiles)
        fs0 = self.front_steps(tiles[0])
        ne = 2 + tiles[0].nblk

        def early():
            for f in fs0[:ne]:
                f()
        self.early_load = early
        conv_head, conv_steps = self.prologue()
        fs0 = fs0[ne:]
        merged = []
        for k, f in enumerate(fs0):
            if k < len(conv_head):
                merged.append(conv_head[k])
            merged.append(f)
        fs0 = merged
        b1_of = {}
        b2_of = {}
        for r in range(n + 2):
            self.cur_round = r
            if r == 0:
                fs = fs0
            else:
                fs = self.front_steps(tiles[r]) if r < n else []
            b1s, b1g = b1_of.get(r - 1, ([], []))
            if r == 0:
                b1s = conv_steps
            b2 = b2_of.get(r - 2, [])
            lists = []
            if b2:
                lists.append((b2, 0.0, B2_END, "C%d" % (r - 2)))
            if b1s:
                lists.append((b1s, 0.0, 1.0 if r == 0 else 0.02, "B%d" % (r - 1)))
            if fs:
                lists.append((fs, B2_END if b2 else 0.0, 1.0, "F%d" % r))
            items = []
            for li, (steps, t0, t1, tg) in enumerate(lists):
                m = len(steps)
                for k, f in enumerate(steps):
                    items.append((t0 + (t1 - t0) * (k + 0.5) / m, li, k, f, tg))
            items.sort(key=lambda z: (z[0], z[1], z[2]))
            cf = self.round_cost.get((r, True), 0)
            cn = self.round_cost.get((r, False), 0)
            self.filler_q = list(b1g)
            self.n_filler0 = len(b1g)
            self.round_base = self.cost_acc.get((r, False), 0)
            nc_left = sum(1 for it in items if it[4].startswith("C"))
            for _, li, k, f, tg in items:
                self.P.cur_tag = "%s.%d" % (tg, k)
                self.c_active = nc_left > 0
                f()
                if tg.startswith("C"):
                    nc_left -= 1
            self.c_active = False
            self.pull_filler(everything=True)
            if r < n:
                b1_of[r], b2_of[r] = self.back_steps(tiles[r])


def build_program():
    nc = bass.Bass("TRN2", target_bir_lowering=False)
    d = {}

    def din(name, shape):
        d[name] = nc.dram_tensor(name, shape, F32, kind="ExternalInput").ap()

    def dout(name, shape):
        d[name] = nc.dram_tensor(name, shape, F32, kind="ExternalOutput").ap()

    din("xp", [NSEQ * SEQ, D])
    din("pp", [NSEQ * SEQ, 256])
    din("xs", [NSS * SL, D])
    din("psm", [NSS * SL, 256])
    din("ck", [NSS, 128, 128])
    din("cv", [NSS, 128, 128])
    din("sc", [NSS, 30, 512])
    din("w_in", [D, 1792])
    din("w_out", [D, D])
    din("w_up", [D, 4 * D])
    din("w_down", [4 * D, D])
    din("w_gate", [D, D])
    din("w_ple", [256, D])
    din("gvec", [4, D])
    din("cvec", [34, 512])
    din("gfin", [D])
    din("sinks", [8])
    dout("yp", [NSEQ * SEQ, D])
    dout("ys", [NSS * SL, D])
    dout("nkp", [NSEQ, 128, 128])
    dout("nvp", [NSEQ, 128, 128])
    dout("ncp", [NSEQ, 30, 512])
    dout("nks", [NSS, SL, 128])
    dout("nvs", [NSS, SL, 128])
    dout("ncs", [NSS, 30, 512])
    d["wscr"] = nc.dram_tensor("wscr", [NPIECE, 128, 4096], BF16, kind="Internal").ap()

    with ExitStack() as es:
        def sbt(name, shape, dt):
            return es.enter_context(nc.sbuf_tensor("sb_" + name, shape, dt))

        sb = {}
        sb["x"] = [sbt("x0", [128, 4, D], F32), sbt("x1", [128, 4, D], F32)]
        sb["p"] = [sbt("p0", [128, 4, 256], F32), sbt("p1", [128, 4, 256], F32)]
        sb["actT"] = sbt("actT", [128, 8, T], BF16)
        sb["actB"] = sbt("actB", [128, 8, T], BF16)
        sb["actC"] = sbt("actC", [128, 8, T], BF16)
        sb["hst"] = [sbt("hst0", [128, D], BF16), sbt("hst1", [128, D], BF16)]
        sb["qT"] = sbt("qT", [128, 4, T], BF16)
        sb["kT"] = sbt("kT", [128, 128 + T], BF16)
        sb["V"] = sbt("V", [128, 10, 64], BF16)
        sb["PT"] = [sbt("PT0", [128, 3, 256], BF16), sbt("PT1", [128, 3, 256], BF16)]
        sb["attnT"] = sbt("attnT", [128, 4, T], BF16)
        sb["rden"] = sbt("rden", [128, 4, 64], F32)
        sb["sq"] = sbt("sq", [128, 2, T], BF16)
        sb["sdT"] = sbt("sdT", [128, T], F32)
        sb["raT"] = sbt("raT", [128, T], BF16)
        sb["tg"] = sbt("tg", [128, 4, T], BF16)
        sb["u2"] = sbt("u2", [128, 4, 30 + T], BF16)
        sb["utail"] = sbt("utail", [128, 2, 4, 30], F32)
        sb["dg"] = sbt("dg", [128, 4, 31, 32], BF16)
        sb["mask32"] = sbt("mask32", [128, 32], F32)
        sb["acc"] = sbt("acc", [128, 4, T], F32)
        sb["mu"] = sbt("mu", [128, T], F32)
        sb["var"] = sbt("var", [128, T], F32)
        sb["hT"] = sbt("hT", [128, 32, T], BF16)
        sb["wsl"] = sbt("wsl", [128, NSLOT, 8, 512], BF16)
        sb["tgate"] = sbt("tgate", [128, 2, 512], F32)
        sb["wple"] = sbt("wple", [128, 4, 512], BF16)
        sb["pbf"] = sbt("pbf", [128, 256], BF16)
        sb["pT"] = sbt("pT", [128, 2, T], BF16)
        sb["gfin"] = sbt("gfin", [128, D], F32)
        sb["ident"] = sbt("ident", [128, 128], BF16)
        sb["identf"] = sbt("identf", [128, 128], F32)
        sb["ones"] = sbt("ones", [128, 128], BF16)
        sb["gw"] = sbt("gw", [128, 8, 4], F32)
        sb["cw"] = sbt("cw", [128, 4, 34], F32)
        sb["cw2"] = sbt("cw2", [128, 4, 2], F32)
        sb["es"] = sbt("es", [128, 4], F32)
        sb["ssq"] = [sbt("ssq%d" % i, [128, 4], F32) for i in range(4)]
        sb["rs"] = [sbt("rs%d" % i, [128, 4], F32) for i in range(4)]
        sb["kvst"] = sbt("kvst", [128, 256], F32)
        sb["ckst"] = sbt("ckst", [128, 128], F32)
        sb["cvst"] = sbt("cvst", [128, 2, 64], F32)
        sb["gvin"] = sb["tgate"].reshape([128, D])
        sb["cs"] = sb["qT"]
        sb["sdT"] = sb["sdT"]
        sb["mu"] = sb["mu"]
        sb["var"] = sb["var"]
        sb["cvin"] = sb["mu"]
        ps = [es.enter_context(nc.psum_tensor("ps%d" % i, [128, 512], F32)) for i in range(8)]

        build_program.sbuf_free = nc.sbuf_bytes_remaining
        B0 = Builder(nc, d, sb, ps, Prog(nc, dry=True), [], True)
        B0.collect = True
        B0.run()
        wseq = []
        B1_ = Builder(nc, d, sb, ps, Prog(nc, dry=False), wseq, True)
        B1_.round_cost = B0.cost_acc
        B1_.run()
        P = Prog(nc, dry=False)
        if os.environ.get("KTAGMAP"):
            P.tagmap = {}
        B = Builder(nc, d, sb, ps, P, wseq, False)
        B.round_cost = B0.cost_acc
        B.run()
        with nc.allow_low_precision("bf16 matmul operands / intermediates by design (fp32 accumulation)"):
            P.finalize_and_emit()
        build_program.stats = P.stats()
        build_program.est = dict(P.eng_free)
        if P.tagmap is not None:
            import json
            json.dump(P.tagmap, open(os.environ["KTAGMAP"], "w"))
    return nc


_QPERM = np.array([h * 256 + G * 64 + dd for G in range(4) for h in range(2) for dd in range(64)])


def kernel(x_prompt, x_sample, p_prompt, p_sample, cache_k, cache_v, state_conv,
           norm_mix, w_in, sinks, conv_w, conv_b, ln_g, ln_b, attn_out_g, conv_out_g, w_out,
           norm_ffn, w_up, w_down, norm_ple, w_ple_gate, w_ple, final_norm):
    f = lambda a: np.ascontiguousarray(np.asarray(a, dtype=np.float32))
    w_in0 = np.asarray(w_in[0], dtype=np.float32)
    w_in_p = f(np.concatenate([w_in0[:, 0:512][:, _QPERM], w_in0[:, 512:768], w_in0[:, 1280:1792], w_in0[:, 768:1280]], axis=1))
    w_out0 = np.asarray(w_out[0], dtype=np.float32)
    w_out_p = f(np.concatenate([w_out0[0:512][_QPERM], w_out0[512:1024]], axis=0))
    gvec = f(np.stack([np.asarray(norm_mix[0]),
                       np.concatenate([np.asarray(attn_out_g[0])[_QPERM], np.asarray(conv_out_g[0])]),
                       np.asarray(norm_ffn[0]), np.asarray(norm_ple[0])]))
    cvec = f(np.concatenate([np.asarray(conv_w[0]), np.asarray(conv_b[0])[None], np.asarray(ln_g[0])[None],
                             np.asarray(ln_b[0])[None]], axis=0))
    shared = {
        "w_in": w_in_p, "w_out": w_out_p, "w_up": f(w_up[0]), "w_down": f(w_down[0]),
        "w_gate": f(w_ple_gate[0]), "w_ple": f(w_ple[0]), "gvec": gvec, "cvec": cvec,
        "gfin": f(final_norm), "sinks": f(sinks[0]),
    }
    xp = np.asarray(x_prompt, dtype=np.float32)
    pp = np.asarray(p_prompt, dtype=np.float32)[0]
    xs = np.asarray(x_sample, dtype=np.float32)
    psm = np.asarray(p_sample, dtype=np.float32)[0]
    ck = np.asarray(cache_k, dtype=np.float32)[0]
    cv = np.asarray(cache_v, dtype=np.float32)[0]
    sc = np.asarray(state_conv, dtype=np.float32)[0]
    in_maps = []
    for c in range(NCORES):
        m = dict(shared)
        m["xp"] = f(xp[c * NSEQ:(c + 1) * NSEQ].reshape(NSEQ * SEQ, D))
        m["pp"] = f(pp[c * NSEQ:(c + 1) * NSEQ].reshape(NSEQ * SEQ, 256))
        m["xs"] = f(xs[c * NSS:(c + 1) * NSS].reshape(NSS * SL, D))
        m["psm"] = f(psm[c * NSS:(c + 1) * NSS].reshape(NSS * SL, 256))
        m["ck"] = f(ck[c * NSS:(c + 1) * NSS].reshape(NSS, 128, 128))
        m["cv"] = f(cv[c * NSS:(c + 1) * NSS].reshape(NSS, 128, 128))
        m["sc"] = f(sc[c * NSS:(c + 1) * NSS])
        in_maps.append(m)
    nc = build_program()
    res = run_bass_kernel_spmd(nc, in_maps, core_ids=list(range(NCORES)))
    r = res.results
    cat = lambda k: np.concatenate([np.asarray(r[c][k]) for c in range(NCORES)], axis=0)
    y_prompt = cat("yp").reshape(32, SEQ, D)
    y_sample = cat("ys").reshape(16, SL, D)
    nkp = cat("nkp").reshape(1, 32, 128, 2, 64)
    nvp = cat("nvp").reshape(1, 32, 128, 2, 64)
    ncp = cat("ncp").reshape(1, 32, 30, 512)
    nks = cat("nks").reshape(1, 16, SL, 2, 64)
    nvs = cat("nvs").reshape(1, 16, SL, 2, 64)
    ncs = cat("ncs").reshape(1, 16, 30, 512)
    return (y_prompt, y_sample, nkp, nvp, ncp, nks, nvs, ncs)
```

```python
import os
import numpy as np
from contextlib import ExitStack
import concourse.bass as bass
import concourse.mybir as mybir
from concourse.bass_utils import run_bass_kernel_spmd

F32 = mybir.dt.float32
BF16 = mybir.dt.bfloat16
AF = mybir.ActivationFunctionType
ALU = mybir.AluOpType

ENGS = ("pe", "act", "dve", "pool", "sp")
NCORES = 8
D = 1024
T = 512
SEQ = 2048
NSEQ = 4
NSS = 2
SL = 16
EPS = 1e-6
NSLOT = 5
NPRE = 4
POOL_CONVERT = False
WAIT_ATTACH = 1
B2_END = 0.25
FILLER = True
FILL_MARGIN = 100.0
BLOCK_NS = 3000.0
SYNC_SAME_ENGINE_WAR = True
PACE = 0.8
SYNC_LAT = 250.0
PE_GHZ = 1.95
NDVE_TAPS = 7


class Op:
    __slots__ = ("eng", "emit", "deps", "needs_sig", "sig", "dma", "waits", "eidx", "tag", "wm", "fseq", "cost", "t_end")

    def __init__(self, eng, emit, dma):
        self.eng = eng
        self.emit = emit
        self.dma = dma
        self.deps = {}
        self.needs_sig = False
        self.sig = None
        self.waits = []
        self.eidx = -1
        self.wm = 0
        self.fseq = 0
        self.cost = 0
        self.t_end = 0.0


class Prog:
    def __init__(self, nc, dry=False):
        self.nc = nc
        self.dry = dry
        self.eng_ops = {e: [] for e in ENGS}
        self.bufs = {}
        self.dma_cnt = {}
        self.final_keys = set()
        self.cur_tag = ""
        self.tagmap = None
        self.filler_mode = False
        self.fifo = []
        self.fseq = 0
        self.credit = 0.0
        self.ratio = 1.0
        self.cost_acc = {}
        self.cost_key = None
        self.eng_free = {e: 0.0 for e in ENGS}

    def ready_time(self, reads, writes, eng=None):
        t = 0.0
        bufs = self.bufs
        for k in reads:
            st = bufs.get(k)
            if st is not None and st[0] is not None:
                o = st[0]
                if o.t_end > t and (o.eng != eng or o.dma is not None):
                    t = o.t_end
        for k in writes:
            st = bufs.get(k)
            if st is not None:
                o = st[0]
                if o is not None and o.t_end > t and (o.eng != eng or o.dma is not None):
                    t = o.t_end
                for r in st[1]:
                    if r.t_end > t and (r.eng != eng or r.dma is not None):
                        t = r.t_end
        return t

    def _append(self, op):
        op.eidx = len(self.eng_ops[op.eng])
        self.eng_ops[op.eng].append(op)

    def flush_fifo(self, upto=None):
        fifo = self.fifo
        n = 0
        while n < len(fifo) and (upto is None or fifo[n].fseq <= upto):
            self._append(fifo[n])
            n += 1
        if n:
            del fifo[:n]

    def add(self, eng, emit, reads=(), writes=(), dma=None, final=False, cost=0, dur=None):
        if self.dry:
            return None
        op = Op(eng, emit, dma)
        op.cost = max(cost, 64)
        op.tag = self.cur_tag
        if dur is not None:
            t0 = self.ready_time(reads, writes, eng) + SYNC_LAT
            ef = self.eng_free[eng]
            if ef > t0:
                t0 = ef
            if dma is not None:
                self.eng_free[eng] = t0 + 60.0
            else:
                self.eng_free[eng] = t0 + dur
            op.t_end = t0 + dur
        bufs = self.bufs
        deps = op.deps
        for k in reads:
            st = bufs.get(k)
            if st is not None and st[0] is not None:
                deps[st[0]] = True
        for k in writes:
            st = bufs.get(k)
            if st is not None:
                if st[0] is not None and st[0] not in deps:
                    deps[st[0]] = False
                for r in st[1]:
                    if r not in deps:
                        deps[r] = False
        for k in reads:
            st = bufs.get(k)
            if st is None:
                bufs[k] = [None, [op]]
            else:
                st[1].append(op)
        for k in writes:
            bufs[k] = [op, []]
        deps.pop(op, None)
        self._append(op)
        if dma is not None:
            self.dma_cnt[dma] = self.dma_cnt.get(dma, 0) + 16
            op.sig = self.dma_cnt[dma]
            if final:
                self.final_keys.add(dma)
        return op

    def finalize_and_emit(self):
        nc = self.nc
        for e in ENGS:
            for op in self.eng_ops[e]:
                best = {}
                for d, raw in op.deps.items():
                    if d.dma is not None:
                        key = ("dma", d.dma)
                        cur = best.get(key)
                        if cur is None or d.sig > cur.sig:
                            best[key] = d
                    else:
                        if d.eng == op.eng and op.dma is None:
                            if op.eng == "pe" or (not raw and not SYNC_SAME_ENGINE_WAR):
                                continue
                        key = ("eng", d.eng)
                        cur = best.get(key)
                        if cur is None or d.eidx > cur.eidx:
                            best[key] = d
                op.deps = list(best.values())
                for d in op.deps:
                    d.needs_sig = True
        self.check_no_deadlock()
        for e in ENGS:
            c = 0
            for op in self.eng_ops[e]:
                if op.dma is None and op.needs_sig:
                    c += 1
                    op.sig = c
        with ExitStack() as es:
            esem = {e: es.enter_context(nc.semaphore("s_" + e)) for e in ENGS if e != "sp"}
            dsem = {}
            for i, k in enumerate(self.dma_cnt):
                dsem[k] = es.enter_context(nc.semaphore("d%d" % i))
            for e in ENGS:
                waited = {}
                for op in self.eng_ops[e]:
                    for d in op.deps:
                        if d.dma is not None:
                            sem = dsem[d.dma]
                            skey = ("d", d.dma)
                        else:
                            sem = esem[d.eng]
                            skey = ("e", d.eng)
                        if waited.get(skey, 0) >= d.sig:
                            continue
                        waited[skey] = d.sig
                        op.waits.append((sem, d.sig))
            block = es.enter_context(nc.Block())
            engmap = {"pe": block.tensor, "act": block.scalar, "dve": block.vector,
                      "pool": block.gpsimd, "sp": block.sync}

            def make(e):
                ops = self.eng_ops[e]
                fin = e == "sp"

                def body(eng):
                    for op in ops:
                        nw = len(op.waits)
                        na = 0 if e == "pe" else WAIT_ATTACH
                        for sem, v in op.waits[:max(0, nw - na)]:
                            eng.wait_ge(sem, v)
                        inst = op.emit(eng)
                        for sem, v in op.waits[max(0, nw - na):]:
                            inst._wait_ge(sem, v)
                        if self.tagmap is not None:
                            self.tagmap[inst.ins.name] = op.tag
                        if op.dma is not None:
                            inst.then_inc(dsem[op.dma], 16)
                        elif op.needs_sig:
                            inst.then_inc(esem[e], 1)
                    if fin:
                        for k in self.final_keys:
                            eng.wait_ge(dsem[k], self.dma_cnt[k])
                return body

            for e in ENGS:
                if self.eng_ops[e] or e == "sp":
                    engmap[e](make(e))

    def stats(self):
        return {e: len(self.eng_ops[e]) for e in ENGS}

    def check_no_deadlock(self):
        ptr = {e: 0 for e in ENGS}
        done = set()
        total = sum(len(v) for v in self.eng_ops.values())
        ndone = 0
        progress = True
        while progress:
            progress = False
            for e in ENGS:
                ops = self.eng_ops[e]
                while ptr[e] < len(ops):
                    op = ops[ptr[e]]
                    if all(id(d) in done for d in op.deps):
                        done.add(id(op))
                        ptr[e] += 1
                        ndone += 1
                        progress = True
                    else:
                        break
        if ndone != total:
            stuck = {e: (ptr[e], len(self.eng_ops[e]), self.eng_ops[e][ptr[e]].tag if ptr[e] < len(self.eng_ops[e]) else None) for e in ENGS}
            raise RuntimeError("schedule deadlock: %r" % (stuck,))


class WStream:
    def __init__(self, builder, seq, dry):
        self.b = builder
        self.seq = seq
        self.dry = dry
        self.pos = 0
        self.loaded = 0
        self.nopen = 0
        self.free = list(range(NSLOT))
        self.slot_of = {}

    def _pump(self):
        while self.loaded < len(self.seq) and self.free:
            m = self.loaded
            if self.seq[m] not in self.b.converted:
                break
            slot = self.free.pop(0)
            self.slot_of[m] = slot
            self.loaded += 1
            self.b.load_piece(self.seq[m], slot)
            assert self.loaded == m + 1, "re-entrant weight pump"

    def preload(self, m):
        assert self.loaded == m and self.pos <= m
        slot = self.free.pop(0)
        self.slot_of[m] = slot
        self.loaded += 1
        return slot

    def get(self, piece):
        self.nopen += 1
        if self.dry:
            n = len(self.seq)
            self.seq.append(piece)
            if n in self.slot_of:
                return (n, self.slot_of.pop(n))
            return (-1, 0)
        n = self.pos
        assert self.seq[n] == piece, (n, self.seq[n], piece)
        self.pos += 1
        self._pump()
        assert self.loaded > n, "weight slot deadlock"
        return (n, self.slot_of.pop(n))

    def done(self, h):
        self.nopen -= 1
        if self.dry:
            if h[0] >= 0:
                self.free.append(h[1])
            return
        self.free.append(h[1])
        self._pump()


class TileDesc:
    pass


def make_tiles():
    tiles = []
    for s in range(NSEQ):
        for i in range(SEQ // T):
            td = TileDesc()
            td.kind = "p"
            td.T = T
            td.nblk = T // 128
            td.bp = 128
            td.tok0 = s * SEQ + i * T
            td.seq = s
            td.first = i == 0
            td.last = i == SEQ // T - 1
            td.L = 64
            td.segs = [0]
            td.nch = T // 64
            tiles.append(td)
    td = TileDesc()
    td.kind = "s"
    td.T = NSS * SL
    td.nblk = 1
    td.bp = NSS * SL
    td.tok0 = 0
    td.seq = 0
    td.first = False
    td.last = True
    td.L = SL
    td.segs = list(range(NSS))
    td.nch = 1
    tiles.append(td)
    for i, td in enumerate(tiles):
        td.idx = i
        td.s = i % 2
    return tiles


P_WIN, P_WOUT, P_WUP, P_WDN, P_WG, P_WPLE = 0, 4, 6, 14, 22, 24
NPIECE = 25
C_Q, C_K, C_V, C_G, C_A = 0, 512, 640, 768, 1280


def wcol(col):
    return P_WIN + col // 512, col % 512


class Builder:
    def __init__(self, nc, dram, sb, ps, P, wseq, dry):
        self.nc = nc
        self.d = dram
        self.sb = sb
        self.ps = ps
        self.psb = [p.bitcast(BF16) for p in ps]
        self.P = P
        self.dry = dry
        self.W = WStream(self, wseq, dry)
        self.fb = 0
        self.bb = 0
        self.ub = 0
        self.c_active = False
        self.no_pull = False
        self.early_load = None
        self.rr = 0
        self.hslot = {}
        self.round_cost = {}
        self.cost_acc = {}
        self.cur_round = 0
        self.collect = False
        self.filler_q = []
        self.in_filler = False
        self.credit = 0.0
        self.ratio = 1.0

    def add(self, eng, emit, reads=(), writes=(), **k):
        if not self.in_filler and not self.collect and not self.P.dry and self.filler_q and not self.no_pull:
            t = self.P.ready_time(reads, writes, eng) + SYNC_LAT
            if eng == "pe" or t - self.P.eng_free["pe"] > BLOCK_NS:
                self.pull_until(t)
        return self.P.add(eng, emit, reads, writes, **k)

    def fbank(self):
        b = self.fb % 4
        self.fb += 1
        return b

    def bbank(self):
        b = 6 + self.bb % 2
        self.bb += 1
        return b

    def ubank(self):
        if self.c_active:
            b = 4 + self.ub % 2
        else:
            b = 4 + self.ub % 4
        self.ub += 1
        return b

    def pe_cost(self, cost):
        cost = max(cost, 64)
        k = (self.cur_round, self.in_filler)
        self.cost_acc[k] = self.cost_acc.get(k, 0) + cost

    def pull_one(self):
        q = self.filler_q
        f, c, _np, _dn = q.pop(0)
        tag = self.P.cur_tag
        self.P.cur_tag = "B%d.g" % (self.cur_round - 1)
        self.in_filler = True
        f()
        self.in_filler = False
        self.P.cur_tag = tag

    def pull_filler(self, everything=False):
        while self.filler_q:
            self.pull_one()

    def pull_until(self, t_ready):
        if self.in_filler or self.collect:
            return
        P = self.P
        q = self.filler_q
        while q and P.eng_free["pe"] + FILL_MARGIN < t_ready:
            if q[0][2] and self.W.nopen >= NSLOT - 2:
                break
            if q[0][3] and self.c_active:
                break
            self.pull_one()

    def pace(self):
        if self.in_filler or self.collect or not self.filler_q:
            return
        k = (self.cur_round, False)
        done = self.cost_acc.get(k, 0) - self.round_base
        tot = self.round_cost.get(k, 0)
        if tot <= 0:
            return
        want = int(self.n_filler0 * min(1.0, PACE * done / tot))
        q = self.filler_q
        while q and (self.n_filler0 - len(q)) < want:
            if q[0][2] and self.W.nopen >= NSLOT - 2:
                break
            if q[0][3] and self.c_active:
                break
            self.pull_one()

    def n_of(self, ap):
        n = 1
        for v in ap.shape[1:]:
            n *= v
        return n

    def mm(self, out, lhsT, rhs, start, stop, reads, writes, tp=None):
        cost = self.n_of(rhs)
        if tp is not None:
            cost = cost // 3 if lhsT.shape[0] <= 32 else cost // 2
        cost = max(cost, 64)
        dur = cost / PE_GHZ + 8.0
        if tp is None:
            self.add("pe", lambda e: e.matmul(out, lhsT=lhsT, rhs=rhs, start=start, stop=stop), reads, writes, dur=dur)
        else:
            self.add("pe", lambda e: e.matmul(out, lhsT=lhsT, rhs=rhs, start=start, stop=stop, tile_position=tp), reads, writes, dur=dur)
        self.pe_cost(cost)
        if stop:
            self.pace()

    def tr(self, out, in_, ident, reads, writes):
        self.add("pe", lambda e: e.transpose(out=out, in_=in_, identity=ident), reads, writes, dur=128 / PE_GHZ + 8.0)
        self.pe_cost(128)

    def act(self, out, in_, func, reads, writes, scale=1.0, bias=0.0, accum=None):
        dur = (self.n_of(out) + 224) / 1.2
        if not isinstance(scale, float) or not isinstance(bias, float):
            dur += 90.0
        if func == AF.Sqrt or func == AF.Exp or func == AF.Tanh:
            dur += 300.0
        if accum is None:
            self.add("act", lambda e: e.activation(out=out, in_=in_, func=func, bias=bias, scale=scale), reads, writes, dur=dur)
        else:
            self.add("act", lambda e: e.activation(out=out, in_=in_, func=func, bias=bias, scale=scale, accum_out=accum), reads, writes, dur=dur + 90.0)

    def vdur(self, eng, out, psum=False):
        n = self.n_of(out)
        if eng == "pool":
            return 100.0 + 2.0 * n
        return (n + (120 if psum else 60)) / 0.96

    def ts(self, eng, out, in0, s1, s2, op0, op1, reads, writes):
        dur = self.vdur(eng, out)
        if s2 is None:
            self.add(eng, lambda e: e.tensor_scalar(out=out, in0=in0, scalar1=s1, scalar2=None, op0=op0), reads, writes, dur=dur)
        else:
            self.add(eng, lambda e: e.tensor_scalar(out=out, in0=in0, scalar1=s1, scalar2=s2, op0=op0, op1=op1), reads, writes, dur=dur)

    def tt(self, eng, out, in0, in1, op, reads, writes):
        self.add(eng, lambda e: e.tensor_tensor(out=out, in0=in0, in1=in1, op=op), reads, writes, dur=self.vdur(eng, out, True))

    def stt(self, eng, out, in0, scalar, in1, op0, op1, reads, writes):
        self.add(eng, lambda e: e.scalar_tensor_tensor(out=out, in0=in0, scalar=scalar, in1=in1, op0=op0, op1=op1), reads, writes,
                 dur=self.vdur(eng, out, True))

    def cp(self, eng, out, in_, reads, writes):
        self.add(eng, lambda e: e.tensor_copy(out=out, in_=in_), reads, writes, dur=self.vdur(eng, out, True))

    def recip(self, out, in_, reads, writes):
        self.add("dve", lambda e: e.reciprocal(out=out, in_=in_), reads, writes, dur=80.0 + 3.0 * self.n_of(out))

    def dma(self, out, in_, reads, writes, key, final=False, slow=False, timed=True):
        dur = None
        if timed:
            nbytes = self.n_of(out) * out.shape[0] * (4 if out.dtype == F32 else 2)
            dur = 2000.0 + nbytes / 150.0
        if slow:
            self.add("sp", lambda e: e.dma_start(out=out, in_=in_, allow_slow_non_contiguous=True), reads, writes, dma=key, final=final, dur=dur)
        else:
            self.add("sp", lambda e: e.dma_start(out=out, in_=in_), reads, writes, dma=key, final=final, dur=dur)

    def wkeys(self, slot):
        return [("w", slot, kc) for kc in range(8)]

    def load_piece(self, piece, slot):
        sb = self.sb
        self.no_pull = True
        self._load_piece(piece, slot)
        self.no_pull = False

    def _load_piece(self, piece, slot):
        sb = self.sb
        self.dma(sb["wsl"][:, slot].rearrange("p k n -> p (k n)"), self.d["wscr"][piece],
                 [("wscr", piece)], self.wkeys(slot), ("wld", slot), timed=False)

    def prologue(self):
        sb, d, ps = self.sb, self.d, self.ps
        identf, ident, ones = sb["identf"], sb["ident"], sb["ones"]
        self.add("pool", lambda e: e.memset(identf[:], 0.0), [], ["identf"])
        self.add("pool", lambda e: e.affine_select(out=identf[:], in_=identf[:], pattern=[[-1, 128]],
                                                   compare_op=ALU.not_equal, fill=1.0, base=0, channel_multiplier=1),
                 ["identf"], ["identf"])
        self.cp("dve", ident[:], identf[:], ["identf"], ["ident"])
        self.add("dve", lambda e: e.memset(ones[:], 1.0), [], ["ones"])
        self.dma(sb["gvin"][0:4, :], d["gvec"], [], [("tgate", 0), ("tgate", 1)], "c_gv")
        self.dma(sb["cvin"][0:34, :], d["cvec"], [], ["mu"], "c_cv")
        self.dma(sb["gfin"][:], d["gfin"].partition_broadcast(128), [], ["gfin"], "c_gf")
        self.dma(sb["es"][0:64, :], d["sinks"][0:4].partition_broadcast(64), [], ["es0"], "c_s0")
        self.dma(sb["es"][64:128, :], d["sinks"][4:8].partition_broadcast(64), [], ["es1"], "c_s1")
        if self.early_load is not None:
            self.early_load()
        self.act(sb["es"][:], sb["es"][:], AF.Exp, ["es0", "es1"], ["es"])
        bk = self.fbank()
        for kc in range(8):
            self.tr(ps[bk][:, kc * 4:(kc + 1) * 4], sb["gvin"][0:4, kc * 128:(kc + 1) * 128], identf[0:4, 0:4],
                    [("tgate", 0), ("tgate", 1), "identf"], [("ps", bk)])
        self.cp("dve", sb["gw"][:].rearrange("p k s -> p (k s)"), ps[bk][:, 0:32], [("ps", bk)], ["gw"])
        bk = self.fbank()
        for c in range(4):
            self.tr(ps[bk][:, c * 34:(c + 1) * 34], sb["cvin"][0:34, c * 128:(c + 1) * 128], identf[0:34, 0:34],
                    ["mu", "identf"], [("ps", bk)])
        self.cp("dve", sb["cw"][:].rearrange("p c s -> p (c s)"), ps[bk][:, 0:136], [("ps", bk)], ["cw"])
        self.ts("dve", sb["cw"][:, :, 0:31], sb["cw"][:, :, 0:31], 0.5, None, ALU.mult, None, ["cw"], ["cw"])
        self.ts("dve", sb["cw2"][:], sb["cw"][:, :, 32:34], 0.5, None, ALU.mult, None, ["cw"], ["cw2"])
        m32 = sb["mask32"]
        self.tt("dve", m32[:], identf[:, 0:32], identf[:, 32:64], ALU.add, ["identf"], ["mask32"])
        self.tt("dve", m32[:], m32[:], identf[:, 64:96], ALU.add, ["identf", "mask32"], ["mask32"])
        self.tt("dve", m32[:], m32[:], identf[:, 96:128], ALU.add, ["identf", "mask32"], ["mask32"])
        for c in range(4):
            self.tt("dve", sb["dg"][:, c], m32[:].unsqueeze(1).to_broadcast([128, 31, 32]),
                    sb["cw"][:, c, 0:31].unsqueeze(2).to_broadcast([128, 31, 32]), ALU.mult, ["mask32", "cw"], ["dg"])
        self.converted = set()
        self.cvn = 0
        self.cv_pending = None
        self.cv_pipe(0, pre_m=0)
        self.cv_pipe(1, pre_m=1)
        head = [(lambda piece=piece: self.cv_pipe(piece, pre_m=piece if piece < NPRE else None)) for piece in range(2, P_WUP)]
        rest = [(lambda piece=piece: self.cv_pipe(piece)) for piece in list(range(P_WUP, P_WPLE)) + [P_WPLE]]
        rest.append(lambda: self.cv_pipe(None))
        return head, rest

    def convert_piece(self, piece, pre_m=None):
        self.cv_finish(self.cv_load(piece, pre_m))

    def cv_pipe(self, piece, pre_m=None):
        ctx = self.cv_load(piece, pre_m) if piece is not None else None
        if self.cv_pending is not None:
            self.cv_finish(self.cv_pending)
        self.cv_pending = ctx

    def cv_load(self, piece, pre_m=None):
        sb, d = self.sb, self.d
        stg = sb["hT"].bitcast(F32).reshape([128, 2, 8, 512])
        half = self.cvn % 2
        self.cvn += 1
        hk = [("hT", c) for c in range(16 * half, 16 * half + 16)]
        nkc, ncols, gset = 8, 512, None
        if piece < P_WOUT:
            cb = (piece - P_WIN) * 512
            ncols = min(512, 1792 - cb)
            src = d["w_in"][:, cb:cb + ncols].rearrange("(kc p) n -> p kc n", p=128)
            gset = 0
        elif piece < P_WUP:
            hf = piece - P_WOUT
            src = d["w_out"][:, hf * 512:(hf + 1) * 512].rearrange("(kc p) n -> p kc n", p=128)
            gset = 1
        elif piece < P_WDN:
            j = piece - P_WUP
            src = d["w_up"][:, j * 512:(j + 1) * 512].rearrange("(kc p) n -> p kc n", p=128)
            gset = 2
        elif piece < P_WG:
            q = piece - P_WDN
            hf, kg = q // 4, q % 4
            src = d["w_down"][kg * 1024:(kg + 1) * 1024, hf * 512:(hf + 1) * 512].rearrange("(kc p) n -> p kc n", p=128)
        elif piece < P_WPLE:
            hf = piece - P_WG
            src = d["w_gate"][:, hf * 512:(hf + 1) * 512].rearrange("(kc p) n -> p kc n", p=128)
            gset = 3
        else:
            nkc = 4
            src = d["w_ple"].rearrange("(kc p) (h n) -> p kc h n", p=128, h=2)
        if piece == P_WPLE:
            for kc in range(2):
                self.dma(stg[:, half, 2 * kc:2 * kc + 2, :], src[:, kc], [], hk, ("wfld", half))
        else:
            self.dma(stg[:, half, 0:nkc, 0:ncols], src, [], hk, ("wfld", half))
        return (piece, pre_m, stg, half, hk, nkc, ncols, gset)

    def cv_finish(self, ctx):
        sb, d = self.sb, self.d
        piece, pre_m, stg, half, hk, nkc, ncols, gset = ctx
        pslot = None
        if pre_m is not None:
            pslot = self.W.preload(pre_m)
        dstt = sb["wple"] if piece == P_WPLE else (sb["actC"] if pslot is None else sb["wsl"][:, pslot])
        if pslot is not None and ncols < 512:
            tail = sb["wsl"][:, pslot, :, ncols:512]
            self.add("pool", lambda e: e.memset(tail, 0.0), [], [("wtail", pslot)])
        for kc in range(nkc):
            eng = "act" if (kc % 2 == 0) else "dve"
            o = dstt[:, kc, 0:ncols]
            i_ = stg[:, half, kc, 0:ncols]
            wk = [("wple", kc)] if piece == P_WPLE else ([("actC", kc, b_) for b_ in range(4)] if pslot is None else [("w", pslot, kc)])
            if POOL_CONVERT and piece >= 2:
                if gset is None:
                    self.cp("pool", o, i_, hk, wk)
                else:
                    g = sb["gw"][:, kc, gset:gset + 1].to_broadcast([128, ncols])
                    self.tt("pool", o, i_, g, ALU.mult, hk + ["gw"], wk)
                continue
            if gset is None:
                if eng == "act":
                    self.act(o, i_, AF.Copy, hk, wk)
                else:
                    self.cp(eng, o, i_, hk, wk)
            else:
                g = sb["gw"][:, kc, gset:gset + 1]
                if eng == "act":
                    self.act(o, i_, AF.Copy, hk + ["gw"], wk, scale=g)
                else:
                    self.ts(eng, o, i_, g, None, ALU.mult, None, hk + ["gw"], wk)
        if piece != P_WPLE:
            if pslot is None:
                rk = [("actC", kc, b_) for kc in range(8) for b_ in range(4)]
                self.dma(d["wscr"][piece], sb["actC"][:].rearrange("p k n -> p (k n)"), rk, [("wscr", piece)], "wst")
            else:
                self.dma(d["wscr"][piece], sb["wsl"][:, pslot].rearrange("p k n -> p (k n)"),
                         self.wkeys(pslot) + [("wtail", pslot)], [("wscr", piece)], ("wst", pslot))
            self.converted.add(piece)
            if not self.dry:
                self.W._pump()

    def tm_norm_stat_block(self, td, site, b, junk, jkey):
        sb = self.sb
        s, bp = td.s, td.bp
        x = sb["x"][s]
        ssq, rs = sb["ssq"][site], sb["rs"][site]
        self.act(junk, x[0:bp, b, :], AF.Square, [("x", s, b)], [jkey, ("ssq", site, b)], accum=ssq[0:bp, b:b + 1])
        self.act(rs[0:bp, b:b + 1], ssq[0:bp, b:b + 1], AF.Sqrt, [("ssq", site, b)], [("rs", site, b)], scale=1.0 / D, bias=EPS)
        self.recip(rs[0:bp, b:b + 1], rs[0:bp, b:b + 1], [("rs", site, b)], [("rs", site, b)])

    def tm_norm_A(self, td, site, b):
        sb = self.sb
        s, bp = td.s, td.bp
        x = sb["x"][s]
        hs = self.rr % 2
        self.rr += 1
        self.hslot[(td.idx, site, b)] = hs
        hst = sb["hst"][hs]
        self.tm_norm_stat_block(td, site, b, hst[0:bp, :], ("hst", hs))
        self.ts("dve", hst[0:bp, :], x[0:bp, b, :], sb["rs"][site][0:bp, b:b + 1], None, ALU.mult, None,
                [("x", s, b), ("rs", site, b)], [("hst", hs)])

    def tm_norm_B(self, td, site, b, dst, dkey, bank, evac_eng):
        sb = self.sb
        bp = td.bp
        hs = self.hslot.pop((td.idx, site, b))
        hst = sb["hst"][hs]
        pb = self.psb[bank]
        for kc in range(8):
            self.tr(pb[:, kc * bp:(kc + 1) * bp], hst[0:bp, kc * 128:(kc + 1) * 128], sb["ident"][0:bp, 0:bp],
                    [("hst", hs), "ident"], [("ps", bank)])
        o = dst[:, :, b * 128:b * 128 + bp]
        i_ = pb[:, 0:8 * bp].rearrange("p (k t) -> p k t", k=8)
        wk = [(dkey, kc, b) for kc in range(8)]
        if evac_eng == "act":
            self.act(o, i_, AF.Copy, [("ps", bank)], wk)
        else:
            self.cp(evac_eng, o, i_, [("ps", bank)], wk)

    def norm_pipeline(self, td, site, dst, dkey, bankfn, evac_eng, pre=None):
        nb = td.nblk
        steps = []
        for k in range(nb + 1):
            def st(k=k):
                if pre is not None and k < len(pre) and pre[k] is not None:
                    pre[k]()
                if k < nb:
                    self.tm_norm_A(td, site, k)
                if k >= 1:
                    self.tm_norm_B(td, site, k - 1, dst, dkey, bankfn(), evac_eng)
            steps.append(st)
        return steps

    def front_steps(self, td):
        sb, d, ps = self.sb, self.d, self.ps
        s, Tt, nb, bp, L = td.s, td.T, td.nblk, td.bp, td.L
        x = sb["x"][s]
        aT = sb["actT"]
        aK = "actT"
        steps = []
        xsrc = d["xp"] if td.kind == "p" else d["xs"]
        psrc = d["pp"] if td.kind == "p" else d["psm"]

        def f_load():
            for b in range(nb):
                self.dma(x[0:bp, b, :], xsrc[td.tok0 + b * 128:td.tok0 + b * 128 + bp, :], [], [("x", s, b)], ("xld", s, b))
            if td.kind == "p" and td.first:
                self.add("pool", lambda e: e.memset(sb["u2"][:, :, 0:30], 0.0), [], [("u2", "halo")])
            if td.kind == "s":
                for sg in td.segs:
                    self.dma(sb["var"][0:30, :], d["sc"][sg], [], ["var"], "scld")
                    bk = self.fbank()
                    for c in range(4):
                        self.tr(ps[bk][:, c * 30:(c + 1) * 30], sb["var"][0:30, c * 128:(c + 1) * 128],
                                sb["identf"][0:30, 0:30], ["var", "identf"], [("ps", bk)])
                    self.act(sb["u2"][:, :, sg * 46:sg * 46 + 30], ps[bk][:, 0:120].rearrange("p (c t) -> p c t", c=4),
                             AF.Copy, [("ps", bk)], [("u2", "halo", sg), ("u2", "halo")] + [("u2", c) for c in range(4)], scale=2.0)
                    self.act(sb["utail"][:, sg, :, 0:14], ps[bk][:, 0:120].rearrange("p (c t) -> p c t", c=4)[:, :, 16:30],
                             AF.Copy, [("ps", bk)], [("utail", sg, "a")], scale=2.0)
        steps.append(f_load)

        steps.extend(self.norm_pipeline(td, 0, aT, aK, self.fbank, "act"))

        def ak(kc):
            return [(aK, kc, b_) for b_ in range(nb)]

        def u2cols(sg):
            return 30 if td.kind == "p" else sg * 46 + 30

        def f_q():
            h = self.W.get(P_WIN + 0)
            wsl = sb["wsl"][:, h[1]]
            for G in range(4):
                bk = self.fbank()
                for kc in range(8):
                    self.mm(ps[bk][:, 0:Tt], wsl[:, kc, G * 128:(G + 1) * 128], aT[:, kc, 0:Tt], kc == 0, kc == 7,
                            [("w", h[1], kc)] + ak(kc), [("ps", bk)])
                self.act(sb["qT"][:, G, 0:Tt], ps[bk][:, 0:Tt], AF.Copy, [("ps", bk)], [("qT", G)])
            self.W.done(h)
        steps.append(f_q)

        def g_chunk(h, c):
            piece, off = wcol(C_G + c * 128)
            wsl = sb["wsl"][:, h[1]]
            bk = self.fbank()
            for kc in range(8):
                self.mm(ps[bk][:, 0:Tt], wsl[:, kc, off:off + 128], aT[:, kc, 0:Tt], kc == 0, kc == 7,
                        [("w", h[1], kc)] + ak(kc), [("ps", bk)])
            self.act(sb["tg"][:, c, 0:Tt], ps[bk][:, 0:Tt], AF.Tanh, [("ps", bk)], [("tg", c)], scale=0.5)

        def a_chunk(h, c):
            piece, off = wcol(C_A + c * 128)
            wsl = sb["wsl"][:, h[1]]
            bk = self.fbank()
            for kc in range(8):
                self.mm(ps[bk][:, 0:Tt], wsl[:, kc, off:off + 128], aT[:, kc, 0:Tt], kc == 0, kc == 7,
                        [("w", h[1], kc)] + ak(kc), [("ps", bk)])
            for sg in td.segs:
                t0 = 0 if td.kind == "p" else sg * L
                tl = Tt if td.kind == "p" else L
                uc = u2cols(sg)
                self.stt("dve", sb["u2"][:, c, uc:uc + tl], sb["tg"][:, c, t0:t0 + tl], 1.0, ps[bk][:, t0:t0 + tl],
                         ALU.add, ALU.mult, [("tg", c), ("ps", bk)], [("u2", c)])
                if td.last:
                    if td.kind == "p":
                        self.stt("dve", sb["utail"][:, 0, c, :], sb["tg"][:, c, Tt - 30:Tt], 1.0, ps[bk][:, Tt - 30:Tt],
                                 ALU.add, ALU.mult, [("tg", c), ("ps", bk)], [("utail", 0, c)])
                    else:
                        self.stt("dve", sb["utail"][:, sg, c, 14:30], sb["tg"][:, c, t0:t0 + tl], 1.0, ps[bk][:, t0:t0 + tl],
                                 ALU.add, ALU.mult, [("tg", c), ("ps", bk)], [("utail", sg, c)])

        def f_kvg():
            h = self.W.get(P_WIN + 1)
            wsl = sb["wsl"][:, h[1]]
            wr = [("w", h[1], kc) for kc in range(8)]
            bk = self.fbank()
            for kc in range(8):
                self.mm(ps[bk][:, 0:Tt], wsl[:, kc, 0:128], aT[:, kc, 0:Tt], kc == 0, kc == 7,
                        [("w", h[1], kc)] + ak(kc), [("ps", bk)])
            self.act(sb["kT"][:, 128:128 + Tt], ps[bk][:, 0:Tt], AF.Copy, [("ps", bk)], [("kT", "body")])
            bk = self.fbank()
            nchunks = td.nch * len(td.segs)
            for ci in range(nchunks):
                t0 = ci * L
                for hh in range(2):
                    for kc in range(8):
                        self.mm(ps[bk][hh * 64:hh * 64 + L, ci * 64:(ci + 1) * 64], aT[:, kc, t0:t0 + L],
                                wsl[:, kc, 128 + hh * 64:128 + (hh + 1) * 64], kc == 0, kc == 7,
                                [("w", h[1], kc)] + ak(kc), [("ps", bk)], tp=(0, hh * 64))
            if td.kind == "p":
                self.cp("dve", sb["V"][:, 2:2 + nchunks, :], ps[bk][:, 0:nchunks * 64].rearrange("p (c d) -> p c d", d=64),
                        [("ps", bk)], [("V", "body")])
            else:
                for hh in range(2):
                    self.cp("dve", sb["V"][hh * 64:hh * 64 + L, 2:2 + nchunks, :],
                            ps[bk][hh * 64:hh * 64 + L, 0:nchunks * 64].rearrange("p (c d) -> p c d", d=64),
                            [("ps", bk)], [("V", "body")])
            if td.last:
                bk = self.fbank()
                m0 = Tt - 128 if td.kind == "p" else 0
                mrows = 128 if td.kind == "p" else Tt
                for kc in range(8):
                    self.mm(ps[bk][0:mrows, 0:256], aT[:, kc, m0:m0 + mrows], wsl[:, kc, 0:256], kc == 0, kc == 7,
                            [("w", h[1], kc)] + ak(kc), [("ps", bk)])
                self.act(sb["kvst"][0:mrows, :], ps[bk][0:mrows, 0:256], AF.Copy, [("ps", bk)], ["kvst"])
                if td.kind == "p":
                    self.dma(d["nkp"][td.seq], sb["kvst"][:, 0:128], ["kvst"], [], "kvst", final=True)
                    self.dma(d["nvp"][td.seq], sb["kvst"][:, 128:256], ["kvst"], [], "kvst", final=True)
                else:
                    for sg in td.segs:
                        self.dma(d["nks"][sg], sb["kvst"][sg * L:(sg + 1) * L, 0:128], ["kvst"], [], "kvst", final=True)
                        self.dma(d["nvs"][sg], sb["kvst"][sg * L:(sg + 1) * L, 128:256], ["kvst"], [], "kvst", final=True)
            g_chunk(h, 0)
            g_chunk(h, 1)
            self.W.done(h)
        steps.append(f_kvg)

        def f_ga():
            h = self.W.get(P_WIN + 2)
            g_chunk(h, 2)
            g_chunk(h, 3)
            a_chunk(h, 0)
            a_chunk(h, 1)
            self.W.done(h)
        steps.append(f_ga)

        def f_a2():
            h = self.W.get(P_WIN + 3)
            a_chunk(h, 2)
            a_chunk(h, 3)
            self.W.done(h)
        steps.append(f_a2)

        chunks = []
        if td.kind == "p":
            for c in range(td.nch):
                ents = []
                for kj in (c - 2, c - 1, c):
                    if kj < 0:
                        if td.first:
                            continue
                        ents.append(((kj + 2) * 64, 64, kj + 2, ("kT", "halo"), ("V", "halo")))
                    else:
                        ents.append((128 + kj * 64, 64, 2 + kj, ("kT", "body"), ("V", "body")))
                chunks.append((0, c * 64, ents))
        else:
            for sg in td.segs:
                ents = [(0, 64, 0, ("kT", "halo"), ("V", "halo")), (64, 64, 1, ("kT", "halo"), ("V", "halo")),
                        (128 + sg * L, L, 2 + sg, ("kT", "body"), ("V", "body"))]
                chunks.append((sg, sg * L, ents))
        N4 = 4 * L
        qk = [("qT", G) for G in range(4)]

        def halo_from_cache(sg):
            self.dma(sb["ckst"][:], d["ck"][sg], [], ["ckst"], "ckld")
            bk = self.fbank()
            self.tr(ps[bk][:, 0:128], sb["ckst"][:], sb["identf"][:], ["ckst", "identf"], [("ps", bk)])
            self.act(sb["kT"][:, 0:128], ps[bk][:, 0:128], AF.Copy, [("ps", bk)], [("kT", "halo")])
            for hh in range(2):
                for c2 in range(2):
                    self.dma(sb["cvst"][hh * 64:(hh + 1) * 64, c2, :], d["cv"][sg, c2 * 64:(c2 + 1) * 64, hh * 64:(hh + 1) * 64],
                             [], ["cvst"], "cvld")
            self.cp("dve", sb["V"][:, 0:2, :], sb["cvst"][:], ["cvst"], [("V", "halo")])

        def att_scores(ci):
            sg, q0, ents = chunks[ci]
            par = ci % 2
            bx, by = 2 * par, 2 * par + 1
            for i, (kc0, nk, vs, kkey, vkey) in enumerate(ents):
                bk = bx if i < 2 else by
                co = (i % 2) * 256
                for hh in range(2):
                    self.mm(ps[bk][hh * 64:hh * 64 + nk, co:co + N4].rearrange("p (g q) -> p g q", g=4),
                            sb["kT"][hh * 64:(hh + 1) * 64, kc0:kc0 + nk],
                            sb["qT"][hh * 64:(hh + 1) * 64, :, q0:q0 + L], True, True,
                            [kkey] + qk, [("ps", bk)], tp=(hh * 64, hh * 64))
            PT = sb["PT"][par]
            for i, (kc0, nk, vs, kkey, vkey) in enumerate(ents):
                bk = bx if i < 2 else by
                co = (i % 2) * 256
                if nk == 64:
                    self.act(PT[:, i, 0:N4], ps[bk][:, co:co + N4], AF.Exp, [("ps", bk)], [("PT", par, i)], scale=0.125)
                else:
                    for hh in range(2):
                        self.act(PT[hh * 64:hh * 64 + nk, i, 0:N4], ps[bk][hh * 64:hh * 64 + nk, co:co + N4], AF.Exp,
                                 [("ps", bk)], [("PT", par, i)], scale=0.125)

        def att_pv(ci):
            sg, q0, ents = chunks[ci]
            par = ci % 2
            bx, by = 2 * par, 2 * par + 1
            PT = sb["PT"][par]
            ne = len(ents)
            for i, (kc0, nk, vs, kkey, vkey) in enumerate(ents):
                for hh in range(2):
                    self.mm(ps[bx][hh * 64:(hh + 1) * 64, 0:N4], sb["V"][hh * 64:hh * 64 + nk, vs, :],
                            PT[hh * 64:hh * 64 + nk, i, 0:N4], i == 0, i == ne - 1,
                            [vkey, ("PT", par, i)], [("ps", bx)], tp=(hh * 64, hh * 64))
            for i, (kc0, nk, vs, kkey, vkey) in enumerate(ents):
                for hh in range(2):
                    self.mm(ps[by][hh * 64:(hh + 1) * 64, 256:256 + N4], sb["ones"][hh * 64:hh * 64 + nk, 0:64],
                            PT[hh * 64:hh * 64 + nk, i, 0:N4], i == 0, i == ne - 1,
                            ["ones", ("PT", par, i)], [("ps", by)], tp=(hh * 64, hh * 64))
            rden = sb["rden"]
            self.tt("dve", rden[:, :, 0:L], ps[by][:, 256:256 + N4].rearrange("p (g q) -> p g q", g=4),
                    sb["es"][:].unsqueeze(2).to_broadcast([128, 4, L]), ALU.add, [("ps", by), "es"], ["rden"])
            self.recip(rden[:, :, 0:L], rden[:, :, 0:L], ["rden"], ["rden"])
            self.tt("dve", sb["attnT"][:, :, q0:q0 + L], ps[bx][:, 0:N4].rearrange("p (g q) -> p g q", g=4),
                    rden[:, :, 0:L], ALU.mult, [("ps", bx), "rden"], [("attnT", q0 // 128)])

        if td.kind == "p":
            def f_att0():
                self.fb = 0
                att_scores(0)
            steps.append(f_att0)
            for ci in range(len(chunks)):
                def f_att(ci=ci):
                    if ci + 1 < len(chunks):
                        att_scores(ci + 1)
                    att_pv(ci)
                steps.append(f_att)
        else:
            for ci in range(len(chunks)):
                def f_atts(ci=ci):
                    halo_from_cache(chunks[ci][0])
                    self.fb = 0
                    att_scores(ci)
                    att_pv(ci)
                steps.append(f_atts)

        attn_keys = [("attnT", j) for j in range((Tt + 127) // 128)]

        def fm_rms(src, skeys_fn, eps):
            bk = self.fbank()
            for c in range(4):
                sl = self.rr % 2
                self.rr += 1
                self.act(sb["sq"][:, sl, 0:Tt], src[:, c, 0:Tt], AF.Square, skeys_fn(c), [("sq", sl)])
                self.mm(ps[bk][:, 0:Tt], sb["ones"][:], sb["sq"][:, sl, 0:Tt], c == 0, c == 3,
                        ["ones", ("sq", sl)], [("ps", bk)])
            self.act(sb["sdT"][:, 0:Tt], ps[bk][:, 0:Tt], AF.Sqrt, [("ps", bk)], ["sdT"], scale=1.0 / 512, bias=eps)
            self.recip(sb["raT"][:, 0:Tt], sb["sdT"][:, 0:Tt], ["sdT"], ["raT"])
            for c in range(4):
                self.tt("dve", src[:, c, 0:Tt], src[:, c, 0:Tt], sb["raT"][:, 0:Tt], ALU.mult,
                        skeys_fn(c) + ["raT"], skeys_fn(c))

        steps.append(lambda: fm_rms(sb["attnT"], lambda c: attn_keys, EPS))

        def conv_chunk(c):
            cw, dg, u2 = sb["cw"], sb["dg"], sb["u2"]
            bk = self.fbank()
            npe = 31 - NDVE_TAPS
            for sg in td.segs:
                t0 = 0 if td.kind == "p" else sg * L
                tl = Tt if td.kind == "p" else L
                ub = 0 if td.kind == "p" else sg * 46
                ukeys = [("u2", c), ("u2", "halo"), ("u2", "halo", sg)]
                for tap in range(npe):
                    for i in range(4):
                        self.mm(ps[bk][32 * i:32 * i + 32, t0:t0 + tl], dg[32 * i:32 * i + 32, c, tap, :],
                                u2[32 * i:32 * i + 32, c, ub + tap:ub + tap + tl], tap == 0, tap == npe - 1,
                                ukeys + ["dg"], [("ps", bk)], tp=(32 * i, 32 * i))
            self.act(sb["acc"][:, c, 0:Tt], ps[bk][:, 0:Tt], AF.Identity, [("ps", bk), "cw"], [("acc", c)], bias=cw[:, c, 31:32])
            for sg in td.segs:
                t0 = 0 if td.kind == "p" else sg * L
                tl = Tt if td.kind == "p" else L
                ub = 0 if td.kind == "p" else sg * 46
                ukeys = [("u2", c), ("u2", "halo"), ("u2", "halo", sg)]
                a = sb["acc"][:, c, t0:t0 + tl]
                for tap in range(npe, 31):
                    self.stt("dve", a, u2[:, c, ub + tap:ub + tap + tl], cw[:, c, tap:tap + 1], a, ALU.mult, ALU.add,
                             ukeys + ["cw", ("acc", c)], [("acc", c)])
        for c in range(4):
            steps.append(lambda c=c: conv_chunk(c))

        def f_newconv():
            for sg in td.segs:
                bk = self.fbank()
                rk = [("utail", sg, c) for c in range(4)] + [("utail", sg, "a"), "identf"]
                for c in range(4):
                    self.tr(ps[bk][0:30, c * 128:(c + 1) * 128], sb["utail"][:, sg, c, :], sb["identf"][:], rk, [("ps", bk)])
                self.act(sb["mu"][0:30, :], ps[bk][0:30, :], AF.Copy, [("ps", bk)], ["mu"], scale=0.5)
                dst = d["ncp"][td.seq] if td.kind == "p" else d["ncs"][sg]
                self.dma(dst, sb["mu"][0:30, :], ["mu"], [], "ncst", final=True)
        if td.last:
            steps.append(f_newconv)

        def f_ln_stats():
            acc = sb["acc"]
            bk1 = self.fbank()
            bk2 = self.fbank()
            for c in range(4):
                sl = self.rr % 2
                self.rr += 1
                self.act(sb["sq"][:, sl, 0:Tt], acc[:, c, 0:Tt], AF.Copy, [("acc", c)], [("sq", sl)])
                self.mm(ps[bk1][:, 0:Tt], sb["ones"][:], sb["sq"][:, sl, 0:Tt], c == 0, c == 3,
                        ["ones", ("sq", sl)], [("ps", bk1)])
                sl = self.rr % 2
                self.rr += 1
                self.act(sb["sq"][:, sl, 0:Tt], acc[:, c, 0:Tt], AF.Square, [("acc", c)], [("sq", sl)])
                self.mm(ps[bk2][:, 0:Tt], sb["ones"][:], sb["sq"][:, sl, 0:Tt], c == 0, c == 3,
                        ["ones", ("sq", sl)], [("ps", bk2)])
            mu, var = sb["mu"], sb["var"]
            self.ts("dve", mu[:, 0:Tt], ps[bk1][:, 0:Tt], 1.0 / 512, None, ALU.mult, None, [("ps", bk1)], ["mu"])
            self.tt("dve", var[:, 0:Tt], mu[:, 0:Tt], mu[:, 0:Tt], ALU.mult, ["mu"], ["var"])
            self.stt("dve", var[:, 0:Tt], ps[bk2][:, 0:Tt], 1.0 / 512, var[:, 0:Tt], ALU.mult, ALU.subtract,
                     [("ps", bk2), "var"], ["var"])
            self.act(var[:, 0:Tt], var[:, 0:Tt], AF.Sqrt, ["var"], ["var"], bias=EPS)
            self.recip(var[:, 0:Tt], var[:, 0:Tt], ["var"], ["var"])
        steps.append(f_ln_stats)

        def ln_chunk(c):
            acc, cw, cw2 = sb["acc"], sb["cw"], sb["cw2"]
            a = acc[:, c, 0:Tt]
            self.tt("dve", a, a, sb["mu"][:, 0:Tt], ALU.subtract, [("acc", c), "mu"], [("acc", c)])
            self.stt("dve", a, a, cw[:, c, 32:33], sb["var"][:, 0:Tt], ALU.mult, ALU.mult, [("acc", c), "cw", "var"], [("acc", c)])
            sl = self.rr % 2
            self.rr += 1
            self.act(sb["sq"][:, sl, 0:Tt], a, AF.Tanh, [("acc", c), "cw2"], [("sq", sl)], scale=0.5, bias=cw2[:, c, 1:2])
            self.ts("dve", a, a, cw[:, c, 33:34], None, ALU.add, None, [("acc", c), "cw"], [("acc", c)])
            self.stt("dve", sb["cs"][:, c, 0:Tt], sb["sq"][:, sl, 0:Tt], 1.0, a, ALU.add, ALU.mult,
                     [("sq", sl), ("acc", c)], [("qT", c)])
        for c in range(4):
            steps.append(lambda c=c: ln_chunk(c))
        steps.append(lambda: fm_rms(sb["cs"], lambda c: [("qT", c)], 4.0 * EPS))

        def f_wout(b):
            if b == 0:
                self.wo_h = [self.W.get(P_WOUT + 0), self.W.get(P_WOUT + 1)]
            for hf in range(2):
                h = self.wo_h[hf]
                wsl = sb["wsl"][:, h[1]]
                bk = self.fbank()
                for kc in range(8):
                    if kc < 4:
                        lhsT = sb["attnT"][:, kc, b * 128:b * 128 + bp]
                        rk = attn_keys
                    else:
                        lhsT = sb["cs"][:, kc - 4, b * 128:b * 128 + bp]
                        rk = [("qT", kc - 4)]
                    self.mm(ps[bk][0:bp, :], lhsT, wsl[:, kc, :], kc == 0, kc == 7,
                            [("w", h[1], kc)] + rk, [("ps", bk)])
                xs = x[0:bp, b, hf * 512:(hf + 1) * 512]
                self.tt("dve", xs, xs, ps[bk][0:bp, :], ALU.add, [("x", s, b), ("ps", bk)], [("x", s, b)])
            if b == nb - 1:
                self.W.done(self.wo_h[0])
                self.W.done(self.wo_h[1])

        pre = [None] + [(lambda b=b: f_wout(b)) for b in range(1, nb)]
        steps.append(lambda: f_wout(0))
        steps.extend(self.norm_pipeline(td, 1, sb["actB"], "actB", self.fbank, "dve", pre=pre))

        def f_halo():
            self.cp("pool", sb["kT"][:, 0:128], sb["kT"][:, Tt:Tt + 128], [("kT", "body")], [("kT", "halo")])
            self.cp("pool", sb["V"][:, 0:2, :], sb["V"][:, 8:10, :], [("V", "body")], [("V", "halo")])
            self.cp("pool", sb["u2"][:, :, 0:30], sb["u2"][:, :, Tt:Tt + 30], [("u2", c) for c in range(4)], [("u2", "halo")])
        if td.kind == "p" and not td.last:
            steps.append(f_halo)
        return steps

    def back_steps(self, td):
        sb, d, ps = self.sb, self.d, self.ps
        s, Tt, nb, bp = td.s, td.T, td.nblk, td.bp
        x = sb["x"][s]
        aB = sb["actB"]
        aK = "actB"
        aC = sb["actC"]
        aCK = "actC"
        steps = []

        def ak(kc):
            return [(aK, kc, b_) for b_ in range(nb)]

        psrc = d["pp"] if td.kind == "p" else d["psm"]

        def f_pload():
            self.dma(sb["p"][s][0:bp, 0:nb, :], psrc[td.tok0:td.tok0 + Tt, :].rearrange("(b p) c -> p b c", p=bp),
                     [], [("p", s)], ("pld", s))
        steps.append(f_pload)

        groups = []

        def up_group(j, m):
            if m == 0:
                self.up_h = self.W.get(P_WUP + j)
            h = self.up_h
            wsl = sb["wsl"][:, h[1]]
            bk = self.ubank()
            for kc in range(8):
                self.mm(ps[bk][:, 0:Tt], wsl[:, kc, m * 128:(m + 1) * 128], aB[:, kc, 0:Tt], kc == 0, kc == 7,
                        [("w", h[1], kc)] + ak(kc), [("ps", bk)])
            hc = sb["hT"][:, j * 4 + m, 0:Tt]
            self.act(hc, ps[bk][:, 0:Tt], AF.Relu, [("ps", bk)], [("hT", j * 4 + m)])
            self.tt("pool", hc, hc, hc, ALU.mult, [("hT", j * 4 + m)], [("hT", j * 4 + m)])
            if m == 3:
                self.W.done(h)
        for j in range(8):
            for m in range(4):
                groups.append((lambda j=j, m=m: up_group(j, m), 8 * max(Tt, 64), m == 0, False))

        def down_group(hf, kg, q):
            if q == 0:
                self.dn_h = self.W.get(P_WDN + hf * 4 + kg)
            h = self.dn_h
            wsl = sb["wsl"][:, h[1]]
            for kc in (2 * q, 2 * q + 1):
                for b in range(nb):
                    self.mm(ps[4 + b][0:bp, :], sb["hT"][:, kg * 8 + kc, b * 128:b * 128 + bp], wsl[:, kc, :],
                            kg == 0 and kc == 0, kg == 3 and kc == 7,
                            [("w", h[1], kc), ("hT", kg * 8 + kc)], [("ps", 4 + b)])
            if q == 3:
                self.W.done(h)
                if kg == 3:
                    for b in range(nb):
                        xs = x[0:bp, b, hf * 512:(hf + 1) * 512]
                        self.tt("dve", xs, xs, ps[4 + b][0:bp, :], ALU.add, [("x", s, b), ("ps", 4 + b)], [("x", s, b)])
        for hf in range(2):
            for kg in range(4):
                for q in range(4):
                    groups.append((lambda hf=hf, kg=kg, q=q: down_group(hf, kg, q), 2 * nb * 512, q == 0, True))

        steps1 = (steps, groups)
        steps = []
        steps.extend(self.norm_pipeline(td, 2, aC, aCK, self.bbank, "dve"))

        def f_pT():
            for b in range(nb):
                self.act(sb["pbf"][0:bp, :], sb["p"][s][0:bp, b, :], AF.Copy, [("p", s)], ["pbf"])
                bk = self.bbank()
                pb = self.psb[bk]
                for k2 in range(2):
                    self.tr(pb[:, k2 * bp:(k2 + 1) * bp], sb["pbf"][0:bp, k2 * 128:(k2 + 1) * 128], sb["ident"][0:bp, 0:bp],
                            ["pbf", "ident"], [("ps", bk)])
                self.act(sb["pT"][:, :, b * 128:b * 128 + bp], pb[:, 0:2 * bp].rearrange("p (k t) -> p k t", k=2),
                         AF.Copy, [("ps", bk)], [("pT", b)])
        steps.append(f_pT)

        def f_ple(hf):
            hg = self.W.get(P_WG + hf)
            wp = sb["wple"]
            wg = sb["wsl"][:, hg[1]]
            for b in range(nb):
                bk = self.bbank()
                for kc in range(8):
                    self.mm(ps[bk][0:bp, :], aC[:, kc, b * 128:b * 128 + bp], wg[:, kc, :], kc == 0, kc == 7,
                            [("w", hg[1], kc), (aCK, kc, b)], [("ps", bk)])
                sl = self.rr % 2
                self.rr += 1
                tgt = sb["tgate"][0:bp, sl, :]
                self.act(tgt, ps[bk][0:bp, :], AF.Tanh, [("ps", bk)], [("tgate", sl)], scale=0.5)
                bk2 = self.bbank()
                for k2 in range(2):
                    self.mm(ps[bk2][0:bp, :], sb["pT"][:, k2, b * 128:b * 128 + bp], wp[:, k2 * 2 + hf, :], k2 == 0, k2 == 1,
                            [("wple", k2 * 2 + hf), ("pT", b)], [("ps", bk2)])
                self.stt("dve", tgt, tgt, 1.0, ps[bk2][0:bp, :], ALU.add, ALU.mult, [("tgate", sl), ("ps", bk2)], [("tgate", sl)])
                xs = x[0:bp, b, hf * 512:(hf + 1) * 512]
                self.stt("dve", xs, tgt, 0.5, xs, ALU.mult, ALU.add, [("tgate", sl), ("x", s, b)], [("x", s, b)])
            self.W.done(hg)
        steps.append(lambda: f_ple(0))
        steps.append(lambda: f_ple(1))

        ydst = d["yp"] if td.kind == "p" else d["ys"]

        def f_out(b):
            xb_ = x[0:bp, b, :]
            sl = self.rr % 2
            self.rr += 1
            self.tm_norm_stat_block(td, 3, b, sb["tgate"].bitcast(BF16)[0:bp, sl, :], ("tgate", sl))
            self.stt("dve", xb_, xb_, sb["rs"][3][0:bp, b:b + 1], sb["gfin"][0:bp, :], ALU.mult, ALU.mult,
                     [("x", s, b), ("rs", 3, b), "gfin"], [("x", s, b)])
            self.dma(ydst[td.tok0 + b * 128:td.tok0 + b * 128 + bp, :], xb_, [("x", s, b)], [], ("yst", s, b), final=True)
        for b in range(nb):
            steps.append(lambda b=b: f_out(b))
        return steps1, steps

    def run(self, interleave=True):
        tiles = make_tiles()
        n = len(tiles)
        fs0 = self.front_steps(tiles[0])
        ne = 2 + tiles[0].nblk

        def early():
            for f in fs0[:ne]:
                f()
        self.early_load = early
        conv_head, conv_steps = self.prologue()
        fs0 = fs0[ne:]
        merged = []
        for k, f in enumerate(fs0):
            if k < len(conv_head):
                merged.append(conv_head[k])
            merged.append(f)
        fs0 = merged
        b1_of = {}
        b2_of = {}
        for r in range(n + 2):
            self.cur_round = r
            if r == 0:
                fs = fs0
            else:
                fs = self.front_steps(tiles[r]) if r < n else []
            b1s, b1g = b1_of.get(r - 1, ([], []))
            if r == 0:
                b1s = conv_steps
            b2 = b2_of.get(r - 2, [])
            lists = []
            if b2:
                lists.append((b2, 0.0, B2_END, "C%d" % (r - 2)))
            if b1s:
                lists.append((b1s, 0.0, 1.0 if r == 0 else 0.02, "B%d" % (r - 1)))
            if fs:
                lists.append((fs, B2_END if b2 else 0.0, 1.0, "F%d" % r))
            items = []
            for li, (steps, t0, t1, tg) in enumerate(lists):
                m = len(steps)
                for k, f in enumerate(steps):
                    items.append((t0 + (t1 - t0) * (k + 0.5) / m, li, k, f, tg))
            items.sort(key=lambda z: (z[0], z[1], z[2]))
            cf = self.round_cost.get((r, True), 0)
            cn = self.round_cost.get((r, False), 0)
            self.filler_q = list(b1g)
            self.n_filler0 = len(b1g)
            self.round_base = self.cost_acc.get((r, False), 0)
            nc_left = sum(1 for it in items if it[4].startswith("C"))
            for _, li, k, f, tg in items:
                self.P.cur_tag = "%s.%d" % (tg, k)
                self.c_active = nc_left > 0
                f()
                if tg.startswith("C"):
                    nc_left -= 1
            self.c_active = False
            self.pull_filler(everything=True)
            if r < n:
                b1_of[r], b2_of[r] = self.back_steps(tiles[r])


def build_program():
    nc = bass.Bass("TRN2", target_bir_lowering=False)
    d = {}

    def din(name, shape):
        d[name] = nc.dram_tensor(name, shape, F32, kind="ExternalInput").ap()

    def dout(name, shape):
        d[name] = nc.dram_tensor(name, shape, F32, kind="ExternalOutput").ap()

    din("xp", [NSEQ * SEQ, D])
    din("pp", [NSEQ * SEQ, 256])
    din("xs", [NSS * SL, D])
    din("psm", [NSS * SL, 256])
    din("ck", [NSS, 128, 128])
    din("cv", [NSS, 128, 128])
    din("sc", [NSS, 30, 512])
    din("w_in", [D, 1792])
    din("w_out", [D, D])
    din("w_up", [D, 4 * D])
    din("w_down", [4 * D, D])
    din("w_gate", [D, D])
    din("w_ple", [256, D])
    din("gvec", [4, D])
    din("cvec", [34, 512])
    din("gfin", [D])
    din("sinks", [8])
    dout("yp", [NSEQ * SEQ, D])
    dout("ys", [NSS * SL, D])
    dout("nkp", [NSEQ, 128, 128])
    dout("nvp", [NSEQ, 128, 128])
    dout("ncp", [NSEQ, 30, 512])
    dout("nks", [NSS, SL, 128])
    dout("nvs", [NSS, SL, 128])
    dout("ncs", [NSS, 30, 512])
    d["wscr"] = nc.dram_tensor("wscr", [NPIECE, 128, 4096], BF16, kind="Internal").ap()

    with ExitStack() as es:
        def sbt(name, shape, dt):
            return es.enter_context(nc.sbuf_tensor("sb_" + name, shape, dt))

        sb = {}
        sb["x"] = [sbt("x0", [128, 4, D], F32), sbt("x1", [128, 4, D], F32)]
        sb["p"] = [sbt("p0", [128, 4, 256], F32), sbt("p1", [128, 4, 256], F32)]
        sb["actT"] = sbt("actT", [128, 8, T], BF16)
        sb["actB"] = sbt("actB", [128, 8, T], BF16)
        sb["actC"] = sbt("actC", [128, 8, T], BF16)
        sb["hst"] = [sbt("hst0", [128, D], BF16), sbt("hst1", [128, D], BF16)]
        sb["qT"] = sbt("qT", [128, 4, T], BF16)
        sb["kT"] = sbt("kT", [128, 128 + T], BF16)
        sb["V"] = sbt("V", [128, 10, 64], BF16)
        sb["PT"] = [sbt("PT0", [128, 3, 256], BF16), sbt("PT1", [128, 3, 256], BF16)]
        sb["attnT"] = sbt("attnT", [128, 4, T], BF16)
        sb["rden"] = sbt("rden", [128, 4, 64], F32)
        sb["sq"] = sbt("sq", [128, 2, T], BF16)
        sb["sdT"] = sbt("sdT", [128, T], F32)
        sb["raT"] = sbt("raT", [128, T], BF16)
        sb["tg"] = sbt("tg", [128, 4, T], BF16)
        sb["u2"] = sbt("u2", [128, 4, 30 + T], BF16)
        sb["utail"] = sbt("utail", [128, 2, 4, 30], F32)
        sb["dg"] = sbt("dg", [128, 4, 31, 32], BF16)
        sb["mask32"] = sbt("mask32", [128, 32], F32)
        sb["acc"] = sbt("acc", [128, 4, T], F32)
        sb["mu"] = sbt("mu", [128, T], F32)
        sb["var"] = sbt("var", [128, T], F32)
        sb["hT"] = sbt("hT", [128, 32, T], BF16)
        sb["wsl"] = sbt("wsl", [128, NSLOT, 8, 512], BF16)
        sb["tgate"] = sbt("tgate", [128, 2, 512], F32)
        sb["wple"] = sbt("wple", [128, 4, 512], BF16)
        sb["pbf"] = sbt("pbf", [128, 256], BF16)
        sb["pT"] = sbt("pT", [128, 2, T], BF16)
        sb["gfin"] = sbt("gfin", [128, D], F32)
        sb["ident"] = sbt("ident", [128, 128], BF16)
        sb["identf"] = sbt("identf", [128, 128], F32)
        sb["ones"] = sbt("ones", [128, 128], BF16)
        sb["gw"] = sbt("gw", [128, 8, 4], F32)
        sb["cw"] = sbt("cw", [128, 4, 34], F32)
        sb["cw2"] = sbt("cw2", [128, 4, 2], F32)
        sb["es"] = sbt("es", [128, 4], F32)
        sb["ssq"] = [sbt("ssq%d" % i, [128, 4], F32) for i in range(4)]
        sb["rs"] = [sbt("rs%d" % i, [128, 4], F32) for i in range(4)]
        sb["kvst"] = sbt("kvst", [128, 256], F32)
        sb["ckst"] = sbt("ckst", [128, 128], F32)
        sb["cvst"] = sbt("cvst", [128, 2, 64], F32)
        sb["gvin"] = sb["tgate"].reshape([128, D])
        sb["cs"] = sb["qT"]
        sb["sdT"] = sb["sdT"]
        sb["mu"] = sb["mu"]
        sb["var"] = sb["var"]
        sb["cvin"] = sb["mu"]
        ps = [es.enter_context(nc.psum_tensor("ps%d" % i, [128, 512], F32)) for i in range(8)]

        build_program.sbuf_free = nc.sbuf_bytes_remaining
        B0 = Builder(nc, d, sb, ps, Prog(nc, dry=True), [], True)
        B0.collect = True
        B0.run()
        wseq = []
        B1_ = Builder(nc, d, sb, ps, Prog(nc, dry=False), wseq, True)
        B1_.round_cost = B0.cost_acc
        B1_.run()
        P = Prog(nc, dry=False)
        if os.environ.get("KTAGMAP"):
            P.tagmap = {}
        B = Builder(nc, d, sb, ps, P, wseq, False)
        B.round_cost = B0.cost_acc
        B.run()
        with nc.allow_low_precision("bf16 matmul operands / intermediates by design (fp32 accumulation)"):
            P.finalize_and_emit()
        build_program.stats = P.stats()
        build_program.est = dict(P.eng_free)
        if P.tagmap is not None:
            import json
            json.dump(P.tagmap, open(os.environ["KTAGMAP"], "w"))
    return nc


_QPERM = np.array([h * 256 + G * 64 + dd for G in range(4) for h in range(2) for dd in range(64)])


def kernel(x_prompt, x_sample, p_prompt, p_sample, cache_k, cache_v, state_conv,
           norm_mix, w_in, sinks, conv_w, conv_b, ln_g, ln_b, attn_out_g, conv_out_g, w_out,
           norm_ffn, w_up, w_down, norm_ple, w_ple_gate, w_ple, final_norm):
    f = lambda a: np.ascontiguousarray(np.asarray(a, dtype=np.float32))
    w_in0 = np.asarray(w_in[0], dtype=np.float32)
    w_in_p = f(np.concatenate([w_in0[:, 0:512][:, _QPERM], w_in0[:, 512:768], w_in0[:, 1280:1792], w_in0[:, 768:1280]], axis=1))
    w_out0 = np.asarray(w_out[0], dtype=np.float32)
    w_out_p = f(np.concatenate([w_out0[0:512][_QPERM], w_out0[512:1024]], axis=0))
    gvec = f(np.stack([np.asarray(norm_mix[0]),
                       np.concatenate([np.asarray(attn_out_g[0])[_QPERM], np.asarray(conv_out_g[0])]),
                       np.asarray(norm_ffn[0]), np.asarray(norm_ple[0])]))
    cvec = f(np.concatenate([np.asarray(conv_w[0]), np.asarray(conv_b[0])[None], np.asarray(ln_g[0])[None],
                             np.asarray(ln_b[0])[None]], axis=0))
    shared = {
        "w_in": w_in_p, "w_out": w_out_p, "w_up": f(w_up[0]), "w_down": f(w_down[0]),
        "w_gate": f(w_ple_gate[0]), "w_ple": f(w_ple[0]), "gvec": gvec, "cvec": cvec,
        "gfin": f(final_norm), "sinks": f(sinks[0]),
    }
    xp = np.asarray(x_prompt, dtype=np.float32)
    pp = np.asarray(p_prompt, dtype=np.float32)[0]
    xs = np.asarray(x_sample, dtype=np.float32)
    psm = np.asarray(p_sample, dtype=np.float32)[0]
    ck = np.asarray(cache_k, dtype=np.float32)[0]
    cv = np.asarray(cache_v, dtype=np.float32)[0]
    sc = np.asarray(state_conv, dtype=np.float32)[0]
    in_maps = []
    for c in range(NCORES):
        m = dict(shared)
        m["xp"] = f(xp[c * NSEQ:(c + 1) * NSEQ].reshape(NSEQ * SEQ, D))
        m["pp"] = f(pp[c * NSEQ:(c + 1) * NSEQ].reshape(NSEQ * SEQ, 256))
        m["xs"] = f(xs[c * NSS:(c + 1) * NSS].reshape(NSS * SL, D))
        m["psm"] = f(psm[c * NSS:(c + 1) * NSS].reshape(NSS * SL, 256))
        m["ck"] = f(ck[c * NSS:(c + 1) * NSS].reshape(NSS, 128, 128))
        m["cv"] = f(cv[c * NSS:(c + 1) * NSS].reshape(NSS, 128, 128))
        m["sc"] = f(sc[c * NSS:(c + 1) * NSS])
        in_maps.append(m)
    nc = build_program()
    res = run_bass_kernel_spmd(nc, in_maps, core_ids=list(range(NCORES)))
    r = res.results
    cat = lambda k: np.concatenate([np.asarray(r[c][k]) for c in range(NCORES)], axis=0)
    y_prompt = cat("yp").reshape(32, SEQ, D)
    y_sample = cat("ys").reshape(16, SL, D)
    nkp = cat("nkp").reshape(1, 32, 128, 2, 64)
    nvp = cat("nvp").reshape(1, 32, 128, 2, 64)
    ncp = cat("ncp").reshape(1, 32, 30, 512)
    nks = cat("nks").reshape(1, 16, SL, 2, 64)
    nvs = cat("nvs").reshape(1, 16, SL, 2, 64)
    ncs = cat("ncs").reshape(1, 16, 30, 512)
    return (y_prompt, y_sample, nkp, nvp, ncp, nks, nvs, ncs)
```
